# Optimizing a Trainium2 kernel written in Bass

```python
import math
import jax, jax.numpy as jnp
from jax import lax
import numpy as np

D_MODEL = 2048
BATCH = 8
SEQ = 2048
DEPTH = 2

GRID_W = 64
CTX_LEN = 256
N_MIXERS = 2
HEAD_DIM = 128
N_HEADS = D_MODEL // HEAD_DIM
N_KV_HEADS = N_HEADS // 4
GROUP = N_HEADS // N_KV_HEADS
WINDOW = 128
BLOCK = 128
ROPE_BASE = 10000.0
CONV_WIDTH = 3
D_FF = 4 * D_MODEL
N_MOD = 6
EPS = 1e-6
NEG = -1e30
N_CONV_LAYERS = (DEPTH + 1) // 2
N_ATTN_LAYERS = DEPTH // 2

kernel_name = "hybrid_shortconv_swa_dit_block"


def _rmsnorm(x, g):
    x32 = x.astype(jnp.float32)
    y = x32 * lax.rsqrt(jnp.mean(x32 * x32, axis=-1, keepdims=True) + EPS)
    return y.astype(x.dtype) * g


def _axial_rope_tables(seq_len):
    rows_n = seq_len // GRID_W
    row = jnp.repeat(jnp.arange(rows_n), GRID_W).astype(jnp.float32)
    col = jnp.tile(jnp.arange(GRID_W), rows_n).astype(jnp.float32)
    nf = HEAD_DIM // 4
    inv_freq = ROPE_BASE ** (-jnp.arange(nf, dtype=jnp.float32) / nf)
    ang_r = row[:, None] * inv_freq[None, :]
    ang_c = col[:, None] * inv_freq[None, :]
    ang = jnp.concatenate([ang_r, ang_r, ang_c, ang_c], axis=-1)
    return jnp.cos(ang), jnp.sin(ang)


def _rotate_half_axial(x):
    shp = x.shape
    xr = x.reshape(shp[:-1] + (2, 2, HEAD_DIM // 4))
    rot = jnp.stack([-xr[..., 1, :], xr[..., 0, :]], axis=-2)
    return rot.reshape(shp)


def _apply_rope(x, cos, sin):
    return x * cos.astype(x.dtype) + _rotate_half_axial(x) * sin.astype(x.dtype)


def _short_conv_mixer(u, w_in, w_conv, w_out):
    b_gate, c_gate, hval = jnp.split(u @ w_in, 3, axis=-1)
    z = c_gate * hval
    zp = jnp.pad(z, ((0, 0), (1, 1), (0, 0)))
    conv = zp[:, :-2] * w_conv[0] + zp[:, 1:-1] * w_conv[1] + zp[:, 2:] * w_conv[2]
    return (b_gate * conv) @ w_out


def _split_qkv(t):
    b, n, _ = t.shape
    q, k, v = jnp.split(t, [N_HEADS * HEAD_DIM, (N_HEADS + N_KV_HEADS) * HEAD_DIM], axis=-1)
    return (q.reshape(b, n, N_KV_HEADS, GROUP, HEAD_DIM),
            k.reshape(b, n, N_KV_HEADS, HEAD_DIM),
            v.reshape(b, n, N_KV_HEADS, HEAD_DIM))


def _window_attention(u, uc, w_qkv, sink, w_o, ctx_out):
    b, s, _ = u.shape
    scale = 1.0 / math.sqrt(HEAD_DIM)
    q, k, v = _split_qkv(u @ w_qkv)
    qc, kc, vc = _split_qkv(uc @ w_qkv)
    cos, sin = _axial_rope_tables(s)
    q = _apply_rope(q, cos[:, None, None, :], sin[:, None, None, :])
    k = _apply_rope(k, cos[:, None, :], sin[:, None, :])

    nb = s // BLOCK
    qb = q.reshape(b, nb, BLOCK, N_KV_HEADS, GROUP, HEAD_DIM)
    pad = ((0, 0), (BLOCK, BLOCK), (0, 0), (0, 0))
    kp = jnp.pad(k, pad).reshape(b, nb + 2, BLOCK, N_KV_HEADS, HEAD_DIM)
    vp = jnp.pad(v, pad).reshape(b, nb + 2, BLOCK, N_KV_HEADS, HEAD_DIM)
    kw = jnp.concatenate([kp[:, :-2], kp[:, 1:-1], kp[:, 2:]], axis=2)
    vw = jnp.concatenate([vp[:, :-2], vp[:, 1:-1], vp[:, 2:]], axis=2)

    blk = jnp.arange(nb)[:, None, None] * BLOCK
    qi = blk + jnp.arange(BLOCK)[None, :, None]
    kj = blk - BLOCK + jnp.arange(3 * BLOCK)[None, None, :]
    valid = (jnp.abs(qi - kj) <= WINDOW) & (kj >= 0) & (kj < s)

    s_loc = jnp.einsum('bnqhgd,bnkhd->bnhgqk', qb, kw).astype(jnp.float32) * scale
    s_loc = jnp.where(valid[None, :, None, None], s_loc, NEG)
    s_ctx = jnp.einsum('bnqhgd,bchd->bnhgqc', qb, kc).astype(jnp.float32) * scale
    sink_h = sink.astype(jnp.float32).reshape(N_KV_HEADS, GROUP)
    s_sink = jnp.broadcast_to(sink_h[None, None, :, :, None, None], s_loc.shape[:-1] + (1,))
    p = jax.nn.softmax(jnp.concatenate([s_loc, s_ctx, s_sink], axis=-1), axis=-1)
    n_loc = 3 * BLOCK
    n_ctx = kc.shape[1]
    p_loc = p[..., :n_loc].astype(v.dtype)
    p_ctx = p[..., n_loc:n_loc + n_ctx].astype(v.dtype)
    o = (jnp.einsum('bnhgqk,bnkhd->bnqhgd', p_loc, vw)
         + jnp.einsum('bnhgqc,bchd->bnqhgd', p_ctx, vc))
    y = o.reshape(b, s, N_HEADS * HEAD_DIM) @ w_o

    yc = None
    if ctx_out:
        sc = jnp.einsum('bqhgd,bkhd->bhgqk', qc, kc).astype(jnp.float32) * scale
        sc_sink = jnp.broadcast_to(sink_h[None, :, :, None, None], sc.shape[:-1] + (1,))
        pc = jax.nn.softmax(jnp.concatenate([sc, sc_sink], axis=-1), axis=-1)[..., :n_ctx]
        oc = jnp.einsum('bhgqk,bkhd->bqhgd', pc.astype(vc.dtype), vc)
        yc = oc.reshape(b, n_ctx, N_HEADS * HEAD_DIM) @ w_o
    return y, yc


def _sq_relu_mlp(u, w1, w2):
    return jnp.square(jax.nn.relu(u @ w1)) @ w2


def setup_inputs(seed: int = 0) -> dict:
    key = jax.random.key(seed)
    ks = jax.random.split(key, 20)
    d = D_MODEL
    qkv_w = (N_HEADS + 2 * N_KV_HEADS) * HEAD_DIM

    def nrm(k, shape, scale):
        return jax.random.normal(k, shape, jnp.float32) * scale

    return {
        "x": nrm(ks[0], (BATCH, SEQ, d), 1.0),
        "c": nrm(ks[1], (BATCH, d), 1.0),
        "ctx": nrm(ks[2], (BATCH, CTX_LEN, d), 1.0),
        "c_ctx": nrm(ks[3], (d,), 1.0),
        "norm1_g": 1.0 + nrm(ks[4], (DEPTH, d), 0.02),
        "norm2_g": 1.0 + nrm(ks[5], (DEPTH, d), 0.02),
        "mod_w": nrm(ks[6], (DEPTH, d, N_MOD * d), 0.5 * d ** -0.5),
        "mod_b": nrm(ks[7], (DEPTH, N_MOD * d), 0.02),
        "conv_w_in": nrm(ks[8], (N_CONV_LAYERS, d, 3 * d), d ** -0.5),
        "conv_w": nrm(ks[9], (N_CONV_LAYERS, CONV_WIDTH, d), CONV_WIDTH ** -0.5),
        "conv_w_out": nrm(ks[10], (N_CONV_LAYERS, d, d), d ** -0.5),
        "attn_w_qkv": nrm(ks[11], (N_ATTN_LAYERS, d, qkv_w), d ** -0.5),
        "attn_sink": nrm(ks[12], (N_ATTN_LAYERS, N_HEADS), 0.5),
        "attn_w_o": nrm(ks[13], (N_ATTN_LAYERS, N_HEADS * HEAD_DIM, d), (N_HEADS * HEAD_DIM) ** -0.5),
        "mlp_w1": nrm(ks[14], (DEPTH, d, D_FF), d ** -0.5),
        "mlp_w2": nrm(ks[15], (DEPTH, D_FF, d), D_FF ** -0.5),
        "final_g": 1.0 + nrm(ks[16], (d,), 0.02),
    }


def reference(x, c, ctx, c_ctx, norm1_g, norm2_g, mod_w, mod_b, conv_w_in, conv_w, conv_w_out,
              attn_w_qkv, attn_sink, attn_w_o, mlp_w1, mlp_w2, final_g):
    h = x
    hc = ctx
    sc_lat = jax.nn.silu(c)
    sc_ctx = jax.nn.silu(c_ctx)
    for i in range(DEPTH):
        last = i == DEPTH - 1
        m = sc_lat @ mod_w[i] + mod_b[i]
        mc = sc_ctx @ mod_w[i] + mod_b[i]
        sh1, s1, g1, sh2, s2, g2 = jnp.split(m[:, None, :], N_MOD, axis=-1)
        csh1, cs1, cg1, csh2, cs2, cg2 = jnp.split(mc, N_MOD, axis=-1)

        u = _rmsnorm(h, norm1_g[i]) * (1.0 + s1) + sh1
        if i % N_MIXERS == 0:
            j = i // N_MIXERS
            y = _short_conv_mixer(u, conv_w_in[j], conv_w[j], conv_w_out[j])
            if not last:
                uc = _rmsnorm(hc, norm1_g[i]) * (1.0 + cs1) + csh1
                yc = _short_conv_mixer(uc, conv_w_in[j], conv_w[j], conv_w_out[j])
        else:
            j = i // N_MIXERS
            uc = _rmsnorm(hc, norm1_g[i]) * (1.0 + cs1) + csh1
            y, yc = _window_attention(u, uc, attn_w_qkv[j], attn_sink[j], attn_w_o[j], not last)
        h = h + g1 * y

        u2 = _rmsnorm(h, norm2_g[i]) * (1.0 + s2) + sh2
        h = h + g2 * _sq_relu_mlp(u2, mlp_w1[i], mlp_w2[i])
        if not last:
            hc = hc + cg1 * yc
            uc2 = _rmsnorm(hc, norm2_g[i]) * (1.0 + cs2) + csh2
            hc = hc + cg2 * _sq_relu_mlp(uc2, mlp_w1[i], mlp_w2[i])
    return _rmsnorm(h, final_g)
```

```python
import contextlib
import math

import numpy as np
import concourse.bass as bass
import concourse.mybir as mybir
from concourse.bass_utils import run_bass_kernel_spmd

F32 = mybir.dt.float32
BF16 = mybir.dt.bfloat16
U8 = mybir.dt.uint8
AF = mybir.ActivationFunctionType
ALU = mybir.AluOpType
AX = mybir.AxisListType

D = 2048
SL = 2048
LC = 256
T = SL + LC
KC = D // 128
DFF = 8192
NH = 16
NKV = 4
EPS = 1e-6
SCALE = 1.0 / math.sqrt(128.0)
MASKV = -30000.0

ENG = ("pe", "act", "dve", "pool", "sp")
EPOCH = 8000
ARENA = 212800


def _merge(dst, src):
    for s, v in src.items():
        if dst.get(s, 0) < v:
            dst[s] = v


class Sch:
    def __init__(self, nc, stack):
        self.nc = nc
        self.stack = stack
        self.ops = {e: [] for e in ENG}
        self.sem_handles = {}
        self.cnt = {}
        self.seen = {e: {} for e in ENG}
        self.res = {}
        self.base = {}
        self.eng_epoch = {e: 0 for e in ENG}
        self.eng_cnt = {e: 0 for e in ENG}
        self.live = {}
        self.ghosts = []

    def sem(self, name):
        if name not in self.sem_handles:
            self.sem_handles[name] = self.stack.enter_context(self.nc.semaphore(name))
            self.cnt[name] = 0
        return self.sem_handles[name]

    @staticmethod
    def _root(key):
        return key[0] if isinstance(key, tuple) else key

    def _deps(self, reads, writes):
        deps = {}
        for r in reads:
            st = self.res.get(r)
            if st and st["w"]:
                _merge(deps, dict([st["w"]]))
        for w in writes:
            st = self.res.get(w)
            if st:
                if st["w"]:
                    _merge(deps, dict([st["w"]]))
                _merge(deps, st["r"])
            else:
                b = self.base.get(self._root(w))
                if b:
                    _merge(deps, b)
        return deps

    def op(self, eng, fn, reads=(), writes=(), dma_sem=None, accum=False):
        if eng == "pe" and not accum:
            for w in writes:
                st = self.res.get(w)
                assert not (st and st["w"] and st["w"][0].startswith("e_pe_") and not st["r"]), f"PE overwrites unread PE result {w}"
        if dma_sem is not None:
            for w in writes:
                st = self.res.get(w)
                if isinstance(w, str) and st and st["w"] and not st["r"]:
                    raise AssertionError(f"DMA overwrites unread buffer {w}")
        deps = self._deps(reads, writes)
        if dma_sem is None:
            if self.eng_cnt[eng] >= EPOCH:
                self.eng_epoch[eng] += 1
                self.eng_cnt[eng] = 0
            sname = f"e_{eng}_{self.eng_epoch[eng]}"
            self.sem(sname)
            self.eng_cnt[eng] += 1
            self.cnt[sname] += 1
            amount = 1
        else:
            sname = dma_sem
            self.sem(sname)
            self.cnt[sname] += 16
            amount = 16
        tok = (sname, self.cnt[sname])
        waits = []
        seen = self.seen[eng]
        for s, v in deps.items():
            if eng == "pe" and s.startswith("e_pe_"):
                continue
            if seen.get(s, 0) >= v:
                continue
            seen[s] = v
            waits.append((s, v))
        self.ops[eng].append((fn, waits, sname, amount))
        for w in writes:
            self.res[w] = {"w": tok, "r": {}}
        for r in reads:
            st = self.res.setdefault(r, {"w": None, "r": {}})
            if st["r"].get(tok[0], 0) < tok[1]:
                st["r"][tok[0]] = tok[1]
        return tok

    def wait_all(self, eng, sems):
        self.ops[eng].append((None, [(s, self.cnt[s]) for s in sems if self.cnt.get(s, 0) > 0], None, 0))

    def set_arena(self, tensor):
        self.arena = tensor

    def carve(self, name, off, shape, dtype):
        dsz = {F32: 4, BF16: 2, U8: 1}[dtype]
        n = 1
        for s in shape:
            n *= s
        nbytes = n * dsz
        assert off % 4 == 0 and off + nbytes <= ARENA, (name, off, nbytes)
        for ln, (lo, ls) in self.live.items():
            assert off + nbytes <= lo or lo + ls <= off, f"carve {name} overlaps live {ln}"
        inh = {}
        keep = []
        for (go, gs, gt) in self.ghosts:
            if off + nbytes <= go or go + gs <= off:
                keep.append((go, gs, gt))
                continue
            _merge(inh, gt)
            if not (off <= go and go + gs <= off + nbytes):
                keep.append((go, gs, gt))
        self.ghosts = keep
        self.base[name] = inh
        self.live[name] = (off, nbytes)
        ap = self.arena[:, off:off + nbytes].bitcast(dtype)
        if len(shape) == 2:
            ap = ap.rearrange("p (a b) -> p a b", a=shape[0], b=shape[1])
        elif len(shape) == 3:
            ap = ap.rearrange("p (a b c) -> p a b c", a=shape[0], b=shape[1], c=shape[2])
        return ap

    def free(self, *names):
        for name in names:
            off, nbytes = self.live.pop(name)
            toks = dict(self.base.pop(name, {}))
            for k in [k for k in self.res if self._root(k) == name]:
                st = self.res.pop(k)
                if st["w"]:
                    _merge(toks, dict([st["w"]]))
                _merge(toks, st["r"])
            self.ghosts.append((off, nbytes, toks))

    def emit(self):
        nc = self.nc
        with nc.Block() as block:
            def body(e):
                def run(engine):
                    for fn, waits, sname, amount in self.ops[e]:
                        for s, v in waits:
                            engine.wait_ge(self.sem_handles[s], v)
                        if fn is None:
                            continue
                        inst = fn(engine)
                        inst.then_inc(self.sem_handles[sname], amount)
                return run

            block.tensor(body("pe"))
            block.scalar(body("act"))
            block.vector(body("dve"))
            block.gpsimd(body("pool"))
            block.sync(body("sp"))


P_OFF = 0
W_OFF = 6144
W_SZ = 65536
IN_OFF = W_OFF + W_SZ
IN_SZ = 98304
TMP_OFF = IN_OFF + IN_SZ
TMP2_OFF = IN_OFF + 73728
MOD_OFF = ARENA - 16384

PHASES = ["p1", "mod", "l0_norm1", "l0_conv", "l0_wout", "l0_norm2", "l0_mlp1", "l0_mlp2",
          "l1_norm1", "l1_qkv", "l1_attn", "l1_wo", "l1_norm2", "l1_mlp1", "l1_mlp2", "final"]


def hres(oc, t0, n):
    return [("hT", oc, j) for j in range(t0 // 256, (t0 + n) // 256)]


def build(dbg_stop=None, dbg_outs=()):
    nc = bass.Bass("TRN2", target_bir_lowering=False)

    def din(name, shape, dt=F32):
        return nc.dram_tensor(name, list(shape), dt, kind="ExternalInput").ap()

    def dscr(name, shape, dt):
        kind = "ExternalOutput" if name in dbg_outs else "Internal"
        return nc.dram_tensor(name, list(shape), dt, kind=kind).ap()

    x_d = din("x", [SL, D])
    ctx_d = din("ctx", [LC, D])
    cT_d = din("cT", [128, KC * 2])
    modw_d = din("mod_w", [2, D, 6 * D])
    mbT_d = din("mod_bT", [128, 2 * 96])
    n1T_d = din("n1T", [128, 2 * KC])
    n2T_d = din("n2T", [128, 2 * KC])
    fgT_d = din("fgT", [128, KC])
    win_d = din("conv_w_in", [D, 3 * D])
    cwT_d = din("conv_wT", [128, KC * 3])
    wout_d = din("conv_w_out", [D, D])
    wqkv_d = din("w_qkv", [D, 3072])
    sink_d = din("sink", [128, NH])
    wo_d = din("w_o", [D, D])
    w1_d = din("w1", [2, D, DFF])
    w2_d = din("w2", [2, DFF, D])
    cos_d = din("cosT", [128, SL])
    sin_d = din("sinT", [128, SL])
    rot_d = din("rotR", [128, 128])
    mask_d = din("masks", [128, 3 * 128])
    id_d = din("ident", [128, 128])
    y_d = nc.dram_tensor("y", [SL, D], F32, kind="ExternalOutput").ap()

    hT = dscr("hT", [D, T], F32)
    vT = dscr("vT", [D, T], BF16)
    hidT = dscr("hidT", [DFF, T], BF16)
    qT = dscr("qT", [D, SL], BF16)
    kT = dscr("kT", [NKV * 128, T], BF16)
    vtok = dscr("vtok", [T, NKV * 128], BF16)
    dbg_u = dscr("dbg_u", [D, T], BF16) if "dbg_u" in dbg_outs else None
    dbg_mod = dscr("dbg_mod", [128, 4 * 96], F32) if "dbg_mod" in dbg_outs else None
    dbg_o = dscr("dbg_o", [D, SL], BF16) if "dbg_o" in dbg_outs else None

    stop_idx = PHASES.index(dbg_stop) if dbg_stop else len(PHASES) - 1

    def active(ph):
        return PHASES.index(ph) <= stop_idx

    with contextlib.ExitStack() as st:
        S = Sch(nc, st)
        arena_t = st.enter_context(nc.sbuf_tensor("arena", [128, ARENA], U8))
        S.set_arena(arena_t)
        ps_t = st.enter_context(nc.psum_tensor("ps", [128, 4096], F32))

        def bank(i, n=512):
            return ps_t[:, i * 512:i * 512 + n]

        ident_f = S.carve("ident_f", 0, [128], F32)
        rotR = S.carve("rotR", 512, [128], F32)
        ident_b = S.carve("ident_b", 1024, [128], BF16)
        ones_b = S.carve("ones_b", 1280, [128], BF16)
        masks = S.carve("masks", 1536, [3, 128], BF16)
        cT = S.carve("cT", 2304, [KC, 2], F32)
        scT = S.carve("scT", 2432, [KC, 2], BF16)
        mbT = S.carve("mbT", 2496, [2, 96], F32)
        modT = S.carve("modT", 3264, [2, 2, 96], F32)
        n1T = S.carve("n1T", 4800, [2, KC], F32)
        n2T = S.carve("n2T", 4928, [2, KC], F32)
        fgT = S.carve("fgT", 5056, [KC], F32)
        Amod = S.carve("Amod", 5120, [4, 2, KC], F32)
        cwT = S.carve("cwT", 5632, [KC, 3], F32)
        sinkT = S.carve("sinkT", 5824, [NH], F32)
        nsinkT = S.carve("nsinkT", 5888, [NH], F32)
        zeroT = S.carve("zeroT", 5952, [KC], F32)

        def sp_load(dst, src, res, sem):
            S.op("sp", lambda e: e.dma_start(out=dst, in_=src), writes=[res], dma_sem=sem)

        sp_load(ident_f, id_d, "ident_f", "c0")
        sp_load(rotR, rot_d, "rotR", "c1")
        sp_load(cT, cT_d.rearrange("p (k two) -> p k two", two=2), "cT", "c2")
        sp_load(mbT, mbT_d.rearrange("p (l j) -> p l j", l=2), "mbT", "c3")
        sp_load(n1T, n1T_d.rearrange("p (l k) -> p l k", l=2), "n1T", "c4")
        sp_load(n2T, n2T_d.rearrange("p (l k) -> p l k", l=2), "n2T", "c5")
        sp_load(fgT, fgT_d, "fgT", "c6")
        sp_load(cwT, cwT_d.rearrange("p (k t) -> p k t", t=3), "cwT", "c7")
        sp_load(sinkT, sink_d, "sinkT", "c8")
        S.op("pool", lambda e: e.dma_start(out=ident_b, in_=id_d), writes=["ident_b"], dma_sem="c9")
        S.op("pool", lambda e: e.dma_start(out=masks, in_=mask_d.rearrange("p (m k) -> p m k", m=3)),
             writes=["masks"], dma_sem="c10")
        S.op("dve", lambda e: e.memset(ones_b, 1.0), writes=["ones_b"])
        S.op("dve", lambda e: e.memset(zeroT, 0.0), writes=["zeroT"])
        S.op("dve", lambda e: e.tensor_scalar(out=nsinkT, in0=sinkT, scalar1=-1.0, scalar2=None, op0=ALU.mult),
             reads=["sinkT"], writes=["nsinkT"])
        S.op("act", lambda e: e.activation(out=scT, in_=cT, func=AF.Silu), reads=["cT"], writes=["scT"])

        wslot_state = {"n": 0}

        def load_slab(w_ap, col0, ncols, kcn, nslots, extra_reads=()):
            i = wslot_state["n"]
            wslot_state["n"] += 1
            slot = i % nslots
            nb = kcn * ncols * 2
            name = f"ws{i}"
            for ln in [ln for ln in S.live if ln.startswith("ws") and ln[2:].isdigit()]:
                lo, ls = S.live[ln]
                o = W_OFF + slot * nb
                if not (o + nb <= lo or lo + ls <= o):
                    S.free(ln)
            view = S.carve(name, W_OFF + slot * nb, [kcn, ncols], BF16)
            src = w_ap[:, col0:col0 + ncols].rearrange("(kc p) n -> p kc n", p=128)
            S.op("pool", lambda e: e.dma_start(out=view, in_=src), reads=list(extra_reads), writes=[name], dma_sem=f"w{slot}_{nb}")
            return name, view

        def free_slabs():
            for ln in [ln for ln in S.live if ln.startswith("ws") and ln[2:].isdigit()]:
                S.free(ln)
            wslot_state["n"] = 0

        psring = {"i": 0}

        def next_bank(nb):
            b = psring["i"] % nb
            psring["i"] += 1
            return b

        def mm_group(pb, wname, wview, c0, in_view, kcn, t0, n, in_res, extra_reads=()):
            def fn(e):
                for kc in range(kcn):
                    inst = e.matmul(bank(pb, n), lhsT=wview[:, kc, c0:c0 + 128], rhs=in_view[:, kc, t0:t0 + n],
                                    start=(kc == 0), stop=(kc == kcn - 1))
                return inst
            S.op("pe", fn, reads=[wname] + list(in_res) + list(extra_reads), writes=[("ps", pb)])

        ms = [S.carve(f"ms{i}", MOD_OFF + i * 8192, [KC, 256], BF16) for i in range(2)]
        mstate = {"n": 0, "slot": {}}

        def mod_dma(l, s_, slots=None):
            i = mstate["n"]
            mstate["n"] += 1
            if slots is None:
                name, view = f"ms{i % 2}", ms[i % 2]
            else:
                name, view = slots[i % len(slots)]
            src = modw_d[l][:, s_ * 256:(s_ + 1) * 256].rearrange("(kc p) n -> p kc n", p=128)
            S.op("pool", lambda e: e.dma_start(out=view, in_=src), writes=[name], dma_sem="d_" + name)
            mstate["slot"][(l, s_)] = (name, view)

        def mod_pe(l, s_):
            name, view = mstate["slot"].pop((l, s_))
            mpb = 6 + l

            def fn(e):
                for c in range(2):
                    j = s_ * 2 + c
                    for kc in range(KC):
                        inst = e.matmul(bank(mpb)[:, 2 * j:2 * j + 2], lhsT=view[:, kc, c * 128:(c + 1) * 128],
                                        rhs=scT[:, kc, :], start=(kc == 0), stop=(kc == KC - 1), skip_group_check=True)
                return inst
            S.op("pe", fn, reads=[name, "scT"], writes=[("ps", mpb)], accum=True)

        def mod_finalize(l, part):
            mpb = 6 + l
            j0, j1 = (0, 32) if part == 0 else (32, 96)
            for kind in range(2):
                srcp = bank(mpb)[:, 0:192].rearrange("p (j two) -> p j two", two=2)[:, j0:j1, kind]
                dst = modT[:, l, kind, j0:j1]
                S.op("dve", lambda e, dst=dst, srcp=srcp: e.tensor_tensor(out=dst, in0=srcp, in1=mbT[:, l, j0:j1], op=ALU.add),
                     reads=[("ps", mpb), "mbT"], writes=[("modT", l, kind, part)])
                which = part
                nT = n1T if which == 0 else n2T
                s_ap = modT[:, l, kind, 16 + 48 * which:32 + 48 * which]
                dstA = Amod[:, l * 2 + kind, which, :]
                S.op("dve", lambda e, dstA=dstA, s_ap=s_ap, nT=nT: e.scalar_tensor_tensor(
                    out=dstA, in0=s_ap, scalar=1.0, in1=nT[:, l, :], op0=ALU.add, op1=ALU.mult),
                    reads=[("modT", l, kind, part), "n1T", "n2T"], writes=[("Amod", l, kind, which)])

        def dump_mod():
            if dbg_mod is not None:
                S.op("sp", lambda e: e.dma_start(out=dbg_mod, in_=modT.rearrange("p l k j -> p (l k j)")),
                     reads=[("modT", l, k, p_) for l in range(2) for k in range(2) for p_ in range(2)], writes=["dbg_mod"], dma_sem="dbg")

        if active("mod"):
            msw = [(f"msw{i}", S.carve(f"msw{i}", W_OFF + i * 8192, [KC, 256], BF16)) for i in range(8)]
            for s_ in range(8):
                mod_dma(0, s_, slots=msw)
        if active("p1"):
            xs = [S.carve(f"xs{i}", IN_OFF + i * 8192, [D], F32) for i in range(4)]
            hst = [S.carve(f"hst{i}", IN_OFF + 32768 + i * 8192, [KC, 128], F32) for i in range(2)]
            for i in range(18):
                src = x_d[i * 128:(i + 1) * 128, :] if i < 16 else ctx_d[(i - 16) * 128:(i - 15) * 128, :]
                sl = i % 2
                xl = i % 4
                S.op("sp", lambda e, src=src, xl=xl: e.dma_start(out=xs[xl], in_=src), writes=[f"xs{xl}"], dma_sem=f"xs{xl}")
                for q in range(4):
                    pb = next_bank(6)

                    def tr(e, q=q, pb=pb, xl=xl):
                        for j in range(4):
                            inst = e.transpose(bank(pb)[:, j * 128:(j + 1) * 128], xs[xl][:, (4 * q + j) * 128:(4 * q + j + 1) * 128], ident_f)
                        return inst
                    S.op("pe", tr, reads=[f"xs{xl}", "ident_f"], writes=[("ps", pb)])
                    dst = hst[sl][:, 4 * q:4 * q + 4, :]
                    srcp = bank(pb).rearrange("p (a b) -> p a b", a=4)
                    if q % 2 == 0:
                        S.op("dve", lambda e, dst=dst, srcp=srcp: e.tensor_copy(out=dst, in_=srcp),
                             reads=[("ps", pb)], writes=[(f"hst{sl}", q)])
                    else:
                        S.op("act", lambda e, dst=dst, srcp=srcp: e.activation(out=dst, in_=srcp, func=AF.Copy),
                             reads=[("ps", pb)], writes=[(f"hst{sl}", q)])
                if i < 16 and active("mod"):
                    mod_pe(0, i)
                    if i + 8 < 16:
                        mod_dma(0, i + 8, slots=msw)
                t0 = i * 128
                S.op("sp", lambda e, sl=sl, t0=t0: e.dma_start(out=hT[:, t0:t0 + 128].rearrange("(kc p) t -> p kc t", p=128), in_=hst[sl]),
                     reads=[(f"hst{sl}", q) for q in range(4)],
                     writes=[r for oc in range(KC) for r in hres(oc, t0 - t0 % 256, 256)], dma_sem=f"hst{sl}")
            S.free("xs0", "xs1", "xs2", "xs3", "hst0", "hst1")

        if active("mod"):
            mod_finalize(0, 0)
            S.free(*[f"msw{i}" for i in range(8)])

        def mod_vec(l, kind, idx, kc):
            return modT[:, l, kind, idx * 16 + kc:idx * 16 + kc + 1]

        def mod_res(l, kind, idx):
            return ("modT", l, kind, 0 if idx < 2 else 1)

        def norm_phase(l, which, ntok, dst_fn, dst_res_fn, final=False, sq_fn=None, hb_offs=None, rs_off=None):
            base = TMP2_OFF
            if hb_offs is None:
                hb_offs = [base + i * 16384 for i in range(3)]
                rs_off = base + 49152
            hb = [S.carve(f"hb{i}", hb_offs[i], [KC, 256], F32) for i in range(3)]
            rs = [S.carve("rs0", rs_off, [256], F32)] * 2
            nsb = ntok // 256
            pbs = {}

            def stage0(sb):
                t0 = sb * 256
                sl = sb % 3
                S.op("sp", lambda e: e.dma_start(out=hb[sl], in_=hT[:, t0:t0 + 256].rearrange("(kc p) t -> p kc t", p=128)),
                     reads=[r for oc in range(KC) for r in hres(oc, t0, 256)], writes=[f"hb{sl}"], dma_sem=f"hb{sl}")

            def stage1(sb):
                sl = sb % 3
                sqv, sqres = sq_fn(sb)
                S.op("act", lambda e: e.activation(out=sqv, in_=hb[sl], func=AF.Square), reads=[f"hb{sl}"], writes=sqres)
                pb = (6 + sb % 2) if final else next_bank(6)
                pbs[sb] = pb

                def ssq(e):
                    for kc in range(KC):
                        inst = e.matmul(bank(pb, 256), lhsT=ones_b, rhs=sqv[:, kc, :], start=(kc == 0), stop=(kc == KC - 1))
                    return inst
                S.op("pe", ssq, reads=sqres + ["ones_b"], writes=[("ps", pb)])

            def stage2(sb):
                t0 = sb * 256
                kind = 1 if t0 >= SL else 0
                sl = sb % 3
                r2 = 0
                pb = pbs[sb]
                S.op("dve", lambda e: e.tensor_scalar(out=rs[r2], in0=bank(pb, 256), scalar1=1.0 / D, scalar2=EPS, op0=ALU.mult, op1=ALU.add),
                     reads=[("ps", pb)], writes=[f"rs{r2}"])
                S.op("act", lambda e: e.activation(out=rs[r2], in_=rs[r2], func=AF.Sqrt), reads=[f"rs{r2}"], writes=[f"rs{r2}"])
                S.op("dve", lambda e: e.reciprocal(out=rs[r2], in_=rs[r2]), reads=[f"rs{r2}"], writes=[f"rs{r2}"])
                S.op("dve", lambda e: e.tensor_tensor(out=hb[sl], in0=hb[sl], in1=rs[r2].unsqueeze(1).broadcast_to([128, KC, 256]), op=ALU.mult),
                     reads=[f"hb{sl}", f"rs{r2}"], writes=[f"hb{sl}"])
                for kc in range(KC):
                    if final:
                        a_ap = fgT[:, kc:kc + 1]
                        b_ap = zeroT[:, kc:kc + 1]
                        mres = ["fgT", "zeroT"]
                    else:
                        a_ap = Amod[:, l * 2 + kind, which, kc:kc + 1]
                        b_ap = mod_vec(l, kind, 3 * which, kc)
                        mres = [("Amod", l, kind, which), mod_res(l, kind, 3 * which)]
                    dst = dst_fn(kc, t0)
                    if kc % 2 == 0:
                        S.op("act", lambda e, kc=kc, dst=dst, a_ap=a_ap, b_ap=b_ap: e.activation(out=dst, in_=hb[sl][:, kc, :], func=AF.Identity, bias=b_ap, scale=a_ap),
                             reads=[f"hb{sl}"] + mres, writes=[dst_res_fn(kc, t0)])
                    else:
                        S.op("dve", lambda e, kc=kc, dst=dst, a_ap=a_ap, b_ap=b_ap: e.tensor_scalar(out=dst, in0=hb[sl][:, kc, :], scalar1=a_ap, scalar2=b_ap, op0=ALU.mult, op1=ALU.add),
                             reads=[f"hb{sl}"] + mres, writes=[dst_res_fn(kc, t0)])
                if final:
                    final_tail(sb)

            for sb in range(min(2, nsb)):
                stage0(sb)
            stage1(0)
            for sb in range(nsb):
                if sb + 2 < nsb:
                    stage0(sb + 2)
                if sb + 1 < nsb:
                    stage1(sb + 1)
                stage2(sb)
            S.free("hb0", "hb1", "hb2", "rs0")

        def dump_u(inT, ntok):
            if dbg_u is not None:
                S.op("sp", lambda e: e.dma_start(out=dbg_u[:, 0:ntok].rearrange("(kc p) t -> p kc t", p=128), in_=inT[:, :, 0:ntok]),
                     reads=in_res(0, ntok), writes=["dbg_u"], dma_sem="dbg")

        def std_norm(l, which, ntok, hb_offs=None, rs_off=None):
            inT = S.carve("inT", IN_OFF, [KC, T], BF16)
            norm_phase(l, which, ntok, lambda kc, t0: inT[:, kc, t0:t0 + 256], lambda kc, t0: ("inT", t0 // 256, kc),
                       sq_fn=lambda sb: (inT[:, :, sb * 256:(sb + 1) * 256], [("inT", sb, kc) for kc in range(KC)]),
                       hb_offs=hb_offs, rs_off=rs_off)
            return inT

        def in_res(t0, n):
            return [("inT", j, kc) for j in range(t0 // 256, (t0 + n + 255) // 256) for kc in range(KC)]

        def tblocks(ntok):
            out = []
            t0 = 0
            while t0 < ntok:
                n = min(512, ntok - t0)
                out.append((t0, n))
                t0 += n
            return out

        def resid_setup(base):
            hold = [S.carve(f"hold{i}", base + i * 2048, [512], F32) for i in range(4)]
            hnew = [S.carve(f"hnew{i}", base + 8192 + i * 2048, [512], F32) for i in range(4)]
            return hold, hnew

        rs = {"i": 0}

        def resid_prefetch(hold, oc, t0, n):
            i = rs["i"]
            rs["i"] += 1
            sl = i % 4
            S.op("sp", lambda e: e.dma_start(out=hold[sl][:, 0:n], in_=hT[oc * 128:(oc + 1) * 128, t0:t0 + n]),
                 reads=hres(oc, t0, n), writes=[f"hold{sl}"], dma_sem=f"hold{sl}")
            return sl

        def resid_apply(hold, hnew, sl, pb, gate_ap, gate_res, oc, t0, n):
            S.op("dve", lambda e: e.scalar_tensor_tensor(out=hnew[sl][:, 0:n], in0=bank(pb, n), scalar=gate_ap, in1=hold[sl][:, 0:n],
                                                        op0=ALU.mult, op1=ALU.add),
                 reads=[("ps", pb), f"hold{sl}"] + gate_res, writes=[f"hnew{sl}"])
            S.op("sp", lambda e: e.dma_start(out=hT[oc * 128:(oc + 1) * 128, t0:t0 + n], in_=hnew[sl][:, 0:n]),
                 reads=[f"hnew{sl}"], writes=hres(oc, t0, n), dma_sem=f"hnew{sl}")

        def gemm_resid(l, w_ap, kcn, ncols_slab, nslots, inT, in_resf, tbs, gate_idx, tmpbase, slabs=None):
            hold, hnew = resid_setup(tmpbase)
            nslab = D // ncols_slab
            cps = ncols_slab // 128
            loaded = {}
            defer2 = (kcn == 64)
            for s in range(min(nslots, nslab)):
                if defer2 and s == 1:
                    continue
                loaded[s] = load_slab(w_ap, s * ncols_slab, ncols_slab, kcn, nslots)
            tiles = [(s, c, tb) for s in range(nslab) for c in range(cps) for tb in tbs]
            PF = 3
            pre = {}
            for i in range(min(PF, len(tiles))):
                s, c, (a0, r0, n) = tiles[i]
                pre[i] = resid_prefetch(hold, s * cps + c, a0, n)
            for i, (s, c, (a0, r0, n)) in enumerate(tiles):
                oc = s * cps + c
                wname, wview = loaded[s]
                pb = next_bank(6)
                mm_group(pb, wname, wview, c * 128, inT, kcn, r0, n, in_resf(r0, n))
                if defer2 and i == 0:
                    loaded[1] = load_slab(w_ap, ncols_slab, ncols_slab, kcn, nslots, extra_reads=[("ps", pb)])
                kind = 1 if a0 >= SL else 0
                resid_apply(hold, hnew, pre[i], pb, mod_vec(l, kind, gate_idx, oc), [mod_res(l, kind, gate_idx)], oc, a0, n)
                if i + PF < len(tiles):
                    s2, c2, (a2, r2, n2) = tiles[i + PF]
                    pre[i + PF] = resid_prefetch(hold, s2 * cps + c2, a2, n2)
                if c == cps - 1 and tb_is_last(tbs, a0) and s + nslots < nslab:
                    loaded[s + nslots] = load_slab(w_ap, (s + nslots) * ncols_slab, ncols_slab, kcn, nslots)
            free_slabs()
            S.free(*[f"hold{i}" for i in range(4)], *[f"hnew{i}" for i in range(4)])

        def tb_is_last(tbs, a0):
            return a0 == tbs[-1][0]

        def mlp(l, ntok):
            tbs = tblocks(ntok)
            E0 = TMP2_OFF + 34816
            rt = [S.carve(f"rt{i}", E0 + i * 2048, [512], F32) for i in range(2)]
            hs = [S.carve(f"hs{i}", E0 + 4096 + i * 4608, [T], BF16) for i in range(2)]
            hsA = [S.carve(f"hsA{i}", E0 + 13312 + i * 1024, [512], BF16) for i in range(2)]
            inT = std_norm(l, 1, ntok, hb_offs=[TMP2_OFF, TMP2_OFF + 16384, W_OFF + 49152], rs_off=TMP2_OFF + 32768)
            dump_u(inT, ntok) if dbg_stop == f"l{l}_norm2" else None
            if not active(f"l{l}_mlp1"):
                S.free("inT", "rt0", "rt1", "hs0", "hs1", "hsA0", "hsA1")
                return
            nslab = DFF // 512
            NS = 3
            loaded = {}
            for s in range(NS):
                loaded[s] = load_slab(w1_d[l], s * 512, 512, KC, NS)
            if l == 0:
                mod_dma(1, 0)
            ri = {"i": 0}

            def epi(pb, n, dst):
                r = ri["i"] % 2
                ri["i"] += 1
                S.op("act", lambda e: e.activation(out=rt[r][:, 0:n], in_=bank(pb, n), func=AF.Relu),
                     reads=[("ps", pb)], writes=[f"rt{r}"])
                return r

            def hid_groups(t0, n):
                return sorted({min(t // 768, 2) for t in (t0, t0 + n - 1)})

            tile_i = 0
            for ti, (t0, n) in enumerate(tbs):
                for ch in range(8):
                    wname, wview = loaded[ch // 4]
                    pb = next_bank(6)
                    mm_group(pb, wname, wview, (ch % 4) * 128, inT, KC, t0, n, in_res(t0, n))
                    r = epi(pb, n, None)
                    hsl = tile_i % 2
                    tile_i += 1
                    S.op("dve", lambda e, r=r, hsl=hsl, n=n: e.tensor_tensor(out=hsA[hsl][:, 0:n], in0=rt[r][:, 0:n], in1=rt[r][:, 0:n], op=ALU.mult),
                         reads=[f"rt{r}"], writes=[f"hsA{hsl}"])
                    S.op("sp", lambda e, hsl=hsl, ch=ch, t0=t0, n=n: e.dma_start(out=hidT[ch * 128:(ch + 1) * 128, t0:t0 + n], in_=hsA[hsl][:, 0:n]),
                         reads=[f"hsA{hsl}"], writes=[("hidT", ch, g) for g in hid_groups(t0, n)], dma_sem=f"hsA{hsl}")
                    if l == 0 and ti == len(tbs) - 1:
                        mod_pe(1, ch)
                        mod_dma(1, ch + 1)
            for s in (3, 4):
                loaded[s] = load_slab(w1_d[l], s * 512, 512, KC, NS)
            ci = 0
            for s in range(2, nslab):
                wname, wview = loaded[s]
                for c in range(4):
                    ch = s * 4 + c
                    hsl = ci % 2
                    ci += 1
                    for ti, (t0, n) in enumerate(tbs):
                        pb = next_bank(6)
                        mm_group(pb, wname, wview, c * 128, inT, KC, t0, n, in_res(t0, n))
                        r = epi(pb, n, None)
                        S.op("dve", lambda e, r=r, hsl=hsl, t0=t0, n=n: e.tensor_tensor(out=hs[hsl][:, t0:t0 + n], in0=rt[r][:, 0:n], in1=rt[r][:, 0:n], op=ALU.mult),
                             reads=[f"rt{r}"], writes=[(f"hs{hsl}", ti)])
                    if l == 0 and ch < 48:
                        mod_pe(1, ch)
                        if ch + 1 < 48:
                            mod_dma(1, ch + 1)
                    S.op("sp", lambda e, hsl=hsl, ch=ch: e.dma_start(out=hidT[ch * 128:(ch + 1) * 128, 0:ntok], in_=hs[hsl][:, 0:ntok]),
                         reads=[(f"hs{hsl}", ti) for ti in range(len(tbs))], writes=[("hidT", ch, g) for g in range(3)], dma_sem=f"hs{hsl}")
                if s >= 2 and s + NS < nslab:
                    loaded[s + NS] = load_slab(w1_d[l], (s + NS) * 512, 512, KC, NS)
            if l == 0:
                mod_finalize(1, 0)
                mod_finalize(1, 1)
                dump_mod()
            free_slabs()
            S.free("rt0", "rt1", "hs0", "hs1", "hsA0", "hsA1", "inT")
            if not active(f"l{l}_mlp2"):
                return
            groups = [(0, 768), (768, 768), (1536, ntok - 1536)]
            for gi, (g0, gn) in enumerate(groups):
                inH = S.carve("inH", IN_OFF, [64, 768], BF16)
                for qk in range(4):
                    S.op("sp", lambda e, qk=qk, g0=g0, gn=gn: e.dma_start(
                        out=inH[:, qk * 16:(qk + 1) * 16, 0:gn],
                        in_=hidT[qk * 2048:(qk + 1) * 2048, g0:g0 + gn].rearrange("(kc p) t -> p kc t", p=128)),
                        reads=[("hidT", ch, gi) for ch in range(qk * 16, (qk + 1) * 16)], writes=[("inH", qk)], dma_sem=f"inH{qk}")
                tbs2 = [(g0 + r0, r0, n) for (r0, n) in tblocks(gn)]
                gemm_resid(l, w2_d[l], 64, 256, 2, inH, lambda r0, n: [("inH", qk) for qk in range(4)], tbs2, 5, TMP_OFF)
                S.free("inH")

        if active("l0_norm1"):
            inT = std_norm(0, 0, T)
            if dbg_stop == "l0_norm1":
                dump_u(inT, T)
        if active("l0_conv"):
            tbs = tblocks(T)
            ctmp = [S.carve(f"ctmp{i}", TMP2_OFF + i * 2048, [512], F32) for i in range(2)]
            zb = [S.carve(f"zb{i}", TMP2_OFF + 4096 + i * 9248, [2312], F32) for i in range(2)]
            cvb = [S.carve(f"cvb{i}", TMP2_OFF + 22592 + i * 9248, [2312], F32) for i in range(2)]
            vst = [S.carve(f"vst{i}", TMP2_OFF + 41088 + i * 4608, [T], BF16) for i in range(2)]
            for i in range(2):
                for pi, (p0, p1) in enumerate(((0, 1), (2049, 2051), (2307, 2312))):
                    S.op("dve", lambda e, i=i, p0=p0, p1=p1: e.memset(zb[i][:, p0:p1], 0.0), writes=[(f"zb{i}", "pad", pi)])

            def zoff(t0):
                return 1 + t0 if t0 < SL else 2051 + (t0 - SL)

            def cvoff(t0):
                return t0 if t0 < SL else 2050 + (t0 - SL)

            slab_cols = []
            for g in range(4):
                slab_cols += [2 * D + g * 512 - D, 2 * D + g * 512, g * 512]
            loaded = {}
            for i in range(4):
                loaded[i] = load_slab(win_d, slab_cols[i], 512, KC, 4)
            nload = 4
            mod_dma(0, 16)

            def b_part(f):
                g, fl = divmod(f, 4)
                zi = f % 2
                wname, wview = loaded[3 * g + 2]
                for ti, (t0, n) in enumerate(tbs):
                    pb = next_bank(6)
                    mm_group(pb, wname, wview, fl * 128, inT, KC, t0, n, in_res(t0, n))
                    S.op("dve", lambda e, pb=pb, zi=zi, t0=t0, n=n: e.tensor_tensor(
                        out=vst[zi][:, t0:t0 + n], in0=bank(pb, n), in1=cvb[zi][:, cvoff(t0):cvoff(t0) + n], op=ALU.mult),
                        reads=[("ps", pb), f"cvb{zi}"], writes=[(f"vst{zi}", ti)])
                S.op("sp", lambda e, zi=zi, f=f: e.dma_start(out=vT[f * 128:(f + 1) * 128, :], in_=vst[zi]),
                     reads=[(f"vst{zi}", ti) for ti in range(len(tbs))], writes=[("vT", f)], dma_sem=f"vst{zi}")

            for f in range(KC):
                g, fl = divmod(f, 4)
                zi = f % 2
                wcn, wcv = loaded[3 * g]
                whn, whv = loaded[3 * g + 1]
                for ti, (t0, n) in enumerate(tbs):
                    pa = next_bank(6)
                    mm_group(pa, wcn, wcv, fl * 128, inT, KC, t0, n, in_res(t0, n))
                    pbk = next_bank(6)
                    mm_group(pbk, whn, whv, fl * 128, inT, KC, t0, n, in_res(t0, n))
                    r = ti % 2
                    S.op("act", lambda e, r=r, pa=pa, n=n: e.activation(out=ctmp[r][:, 0:n], in_=bank(pa, n), func=AF.Copy),
                         reads=[("ps", pa)], writes=[f"ctmp{r}"])
                    S.op("dve", lambda e, r=r, pbk=pbk, zi=zi, t0=t0, n=n: e.tensor_tensor(
                        out=zb[zi][:, zoff(t0):zoff(t0) + n], in0=ctmp[r][:, 0:n], in1=bank(pbk, n), op=ALU.mult),
                        reads=[f"ctmp{r}", ("ps", pbk)], writes=[(f"zb{zi}", ti)])
                zres = [(f"zb{zi}", ti) for ti in range(len(tbs))] + [(f"zb{zi}", "pad", pi) for pi in range(3)] + ["cwT"]
                NC_ = 2306
                S.op("dve", lambda e, zi=zi, f=f: e.tensor_scalar(out=cvb[zi][:, 0:NC_], in0=zb[zi][:, 1:1 + NC_], scalar1=cwT[:, f, 1:2], scalar2=None, op0=ALU.mult),
                     reads=zres, writes=[f"cvb{zi}"])
                S.op("dve", lambda e, zi=zi, f=f: e.scalar_tensor_tensor(out=cvb[zi][:, 0:NC_], in0=zb[zi][:, 0:NC_], scalar=cwT[:, f, 0:1], in1=cvb[zi][:, 0:NC_], op0=ALU.mult, op1=ALU.add),
                     reads=zres + [f"cvb{zi}"], writes=[f"cvb{zi}"])
                S.op("dve", lambda e, zi=zi, f=f: e.scalar_tensor_tensor(out=cvb[zi][:, 0:NC_], in0=zb[zi][:, 2:2 + NC_], scalar=cwT[:, f, 2:3], in1=cvb[zi][:, 0:NC_], op0=ALU.mult, op1=ALU.add),
                     reads=zres + [f"cvb{zi}"], writes=[f"cvb{zi}"])
                for u_ in (16 + 2 * f, 17 + 2 * f):
                    mod_pe(0, u_)
                    if u_ + 1 < 48:
                        mod_dma(0, u_ + 1)
                if f >= 1:
                    b_part(f - 1)
                    if (f - 1) % 4 == 3:
                        while nload < 12 and nload < 3 * ((f - 1) // 4 + 1) + 4:
                            loaded[nload] = load_slab(win_d, slab_cols[nload], 512, KC, 4)
                            nload += 1
                if fl == 3 and nload < 12:
                    while nload < 12 and nload < 3 * g + 6:
                        loaded[nload] = load_slab(win_d, slab_cols[nload], 512, KC, 4)
                        nload += 1
            b_part(KC - 1)
            mod_finalize(0, 1)
            free_slabs()
            S.free("ctmp0", "ctmp1", "zb0", "zb1", "cvb0", "cvb1", "vst0", "vst1", "inT")
        elif active("l0_norm1"):
            S.free("inT")

        def load_inT_from(src, ntok, res_prefix):
            inT = S.carve("inT", IN_OFF, [KC, T], BF16)
            for ti, (t0, n) in enumerate(tblocks(ntok)):
                S.op("sp", lambda e, t0=t0, n=n: e.dma_start(out=inT[:, :, t0:t0 + n], in_=src[:, t0:t0 + n].rearrange("(kc p) t -> p kc t", p=128)),
                     reads=[(res_prefix, f) for f in range(KC)], writes=in_res(t0, n), dma_sem=f"inT{ti}")
            return inT

        if active("l0_wout"):
            inT = load_inT_from(vT, T, "vT")
            tbs3 = [(t0, t0, n) for (t0, n) in tblocks(T)]
            gemm_resid(0, wout_d, KC, 512, 4, inT, in_res, tbs3, 2, TMP2_OFF)
            S.free("inT")
        if active("l0_norm2"):
            mlp(0, T)

        if active("l1_norm1"):
            inT = std_norm(1, 0, T)
            if dbg_stop == "l1_norm1":
                dump_u(inT, T)
        if active("l1_qkv"):
            cosS = S.carve("cosS", TMP2_OFF, [SL], F32)
            sinS = S.carve("sinS", TMP2_OFF + 8192, [SL], F32)
            sp_load(cosS, cos_d, "cosS", "cosS")
            sp_load(sinS, sin_d, "sinS", "sinS")
            qsb = [S.carve(f"qsb{i}", TMP2_OFF + 16384 + i * 2048, [512], F32) for i in range(2)]
            t1 = [S.carve(f"t1{i}", TMP2_OFF + 20480 + i * 2048, [512], F32) for i in range(2)]
            t2 = [S.carve(f"t2{i}", TMP2_OFF + 24576 + i * 2048, [512], F32) for i in range(2)]
            qst = [S.carve(f"qst{i}", TMP2_OFF + 28672 + i * 4608, [T], BF16) for i in range(2)]
            vtst = [S.carve(f"vtst{i}", TMP2_OFF + 37888 + i * 1024, [512], BF16) for i in range(2)]
            loaded = {}
            for s in range(4):
                loaded[s] = load_slab(wqkv_d, s * 512, 512, KC, 4)
            tbs = tblocks(T)
            ri = 0
            rope_pend = {"f": None}

            def flush_rope():
                if rope_pend["f"] is not None:
                    f_ = rope_pend["f"]
                    rope_pend["f"] = None
                    f_()

            def store_head(hc, qs, is_k):
                if is_k:
                    kh = hc - 16
                    S.op("sp", lambda e: e.dma_start(out=kT[kh * 128:(kh + 1) * 128, :], in_=qst[qs]),
                         reads=[(f"qst{qs}", ti) for ti in range(5)], writes=[("kT", kh)], dma_sem=f"qst{qs}")
                else:
                    S.op("sp", lambda e: e.dma_start(out=qT[hc * 128:(hc + 1) * 128, :], in_=qst[qs][:, 0:SL]),
                         reads=[(f"qst{qs}", ti) for ti in range(4)], writes=[("qT", hc)], dma_sem=f"qst{qs}")

            for s in range(5):
                wname, wview = loaded[s]
                for c in range(4):
                    hc = s * 4 + c
                    is_k = hc >= 16
                    qs = hc % 2
                    ntb = 5 if is_k else 4
                    for ti in range(ntb):
                        t0, n = tbs[ti]
                        pb = next_bank(6)
                        mm_group(pb, wname, wview, c * 128, inT, KC, t0, n, in_res(t0, n))
                        if ti == 4:
                            S.op("act", lambda e, pb=pb, qs=qs, t0=t0, n=n: e.activation(out=qst[qs][:, t0:t0 + n], in_=bank(pb, n), func=AF.Copy),
                                 reads=[("ps", pb)], writes=[(f"qst{qs}", ti)])
                            continue
                        r = ri % 2
                        ri += 1
                        qscale = 1.0 if is_k else SCALE
                        S.op("act", lambda e, pb=pb, r=r, qscale=qscale: e.activation(out=qsb[r], in_=bank(pb), func=AF.Copy, scale=qscale),
                             reads=[("ps", pb)], writes=[f"qsb{r}"])
                        flush_rope()

                        def rope_tail(r=r, qs=qs, t0=t0, ti=ti, hc=hc, is_k=is_k):
                            pr = next_bank(6)
                            S.op("pe", lambda e: e.matmul(bank(pr), lhsT=rotR, rhs=qsb[r], start=True, stop=True),
                                 reads=["rotR", f"qsb{r}"], writes=[("ps", pr)])
                            S.op("dve", lambda e: e.tensor_tensor(out=t1[r], in0=qsb[r], in1=cosS[:, t0:t0 + 512], op=ALU.mult),
                                 reads=[f"qsb{r}", "cosS"], writes=[f"t1{r}"])
                            S.op("dve", lambda e: e.tensor_tensor(out=t2[r], in0=bank(pr), in1=sinS[:, t0:t0 + 512], op=ALU.mult),
                                 reads=[("ps", pr), "sinS"], writes=[f"t2{r}"])
                            S.op("dve", lambda e: e.tensor_tensor(out=qst[qs][:, t0:t0 + 512], in0=t1[r], in1=t2[r], op=ALU.add),
                                 reads=[f"t1{r}", f"t2{r}"], writes=[(f"qst{qs}", ti)])
                            if ti == 3:
                                store_head(hc, qs, is_k)
                        rope_pend["f"] = rope_tail
                if s + 4 < 6:
                    loaded[s + 4] = load_slab(wqkv_d, (s + 4) * 512, 512, KC, 4)
            flush_rope()
            wname, wview = loaded[5]
            for tt in range(18):
                pb = next_bank(6)

                def vmm(e, pb=pb, tt=tt, wview=wview):
                    for kc in range(KC):
                        inst = e.matmul(bank(pb), lhsT=inT[:, kc, tt * 128:(tt + 1) * 128], rhs=wview[:, kc, :], start=(kc == 0), stop=(kc == KC - 1))
                    return inst
                S.op("pe", vmm, reads=[wname] + in_res(tt * 128, 128), writes=[("ps", pb)])
                vs = tt % 2
                S.op("act", lambda e, pb=pb, vs=vs: e.activation(out=vtst[vs], in_=bank(pb), func=AF.Copy), reads=[("ps", pb)], writes=[f"vtst{vs}"])
                S.op("sp", lambda e, vs=vs, tt=tt: e.dma_start(out=vtok[tt * 128:(tt + 1) * 128, :], in_=vtst[vs]),
                     reads=[f"vtst{vs}"], writes=[("vtok", tt)], dma_sem=f"vtst{vs}")
            free_slabs()
            S.free("cosS", "sinS", "qsb0", "qsb1", "t10", "t11", "t20", "t21", "qst0", "qst1", "vtst0", "vtst1", "inT")
        elif active("l1_norm1"):
            S.free("inT")

        if active("l1_attn"):
            oT = S.carve("inT", IN_OFF, [KC, T], BF16)
            kS = S.carve("kS", W_OFF, [NKV, T], BF16)
            vS = S.carve("vS", W_OFF + 18432, [18, 512], BF16)
            S.op("sp", lambda e: e.dma_start(out=kS, in_=kT.rearrange("(g p) t -> p g t", p=128)),
                 reads=[("kT", g) for g in range(NKV)], writes=["kS"], dma_sem="kS")
            S.op("sp", lambda e: e.dma_start(out=vS, in_=vtok.rearrange("(tt p) c -> p tt c", p=128)),
                 reads=[("vtok", tt) for tt in range(18)], writes=["vS"], dma_sem="vS")
            qg = [S.carve(f"qg{i}", TMP2_OFF + i * 16384, [4, SL], BF16) for i in range(2)]
            pf = [S.carve(f"pf{i}", TMP2_OFF + 32768 + i * 2592, [648], F32) for i in range(2)]
            pbf = [S.carve(f"pbf{i}", TMP2_OFF + 37952 + i * 1280, [640], BF16) for i in range(2)]
            ptSb = [S.carve(f"ptS{i}", W_OFF + 36864 + i * 5120, [5, 512], BF16) for i in range(2)]
            stat = S.carve("stat", TMP2_OFF + 45632, [4, 8], F32)
            ones_f = S.carve("ones_f", TMP2_OFF + 45760, [128], F32)
            S.op("dve", lambda e: e.memset(ones_f, 1.0), writes=["ones_f"])
            PT_BANK = 4

            def ptH(b):
                return ps_t[:, b * 512:b * 512 + 320].bitcast(BF16).rearrange("p (kt q) -> p kt q", kt=5)
            OB = 7
            heads = [(g, n, hh) for g in range(NKV) for n in range(16) for hh in range(4)]
            NHD = len(heads)

            def load_q(g):
                gs = g % 2
                S.op("sp", lambda e: e.dma_start(out=qg[gs], in_=qT[g * 512:(g + 1) * 512, :].rearrange("(h p) t -> p h t", p=128)),
                     reads=[("qT", g * 4 + hh) for hh in range(4)], writes=[f"qg{gs}"], dma_sem=f"qg{gs}")

            def keyblocks(n):
                kb = [max(n - 1, 0), n, min(n + 1, 15)]
                mk = [2 if n == 0 else 0, None, 2 if n == 15 else 1]
                return kb, mk

            def st_A(i):
                g, n, hh = heads[i]
                if n == 0 and hh == 0 and g + 1 < NKV:
                    load_q(g + 1)
                gs = g % 2
                sb0 = 2 * (i % 2)
                kb, mk = keyblocks(n)

                def smm(e):
                    q_l = qg[gs][:, hh, n * 128:(n + 1) * 128]
                    A = ps_t[:, sb0 * 512:sb0 * 512 + 512]
                    Bk = ps_t[:, (sb0 + 1) * 512:(sb0 + 1) * 512 + 128]
                    if 0 < n < 15:
                        e.matmul(A[:, 0:384], lhsT=q_l, rhs=kS[:, g, (n - 1) * 128:(n + 2) * 128], start=True, stop=False, skip_group_check=True)
                    else:
                        for pos in range(3):
                            e.matmul(A[:, pos * 128:(pos + 1) * 128], lhsT=q_l, rhs=kS[:, g, kb[pos] * 128:(kb[pos] + 1) * 128],
                                     start=(pos == 0), stop=False, skip_group_check=True)
                    e.matmul(A[:, 384:512], lhsT=q_l, rhs=kS[:, g, SL:SL + 128], start=False, stop=False, skip_group_check=True)
                    for pos in (0, 2):
                        if mk[pos] is not None:
                            e.matmul(A[:, pos * 128:(pos + 1) * 128], lhsT=ident_b, rhs=masks[:, mk[pos], :], start=False, stop=(pos == 2), skip_group_check=True)
                    e.matmul(Bk, lhsT=q_l, rhs=kS[:, g, SL + 128:SL + 256], start=True, stop=False, skip_group_check=True)
                    hh_ = g * 4 + hh
                    return e.matmul(ps_t[:, (sb0 + 1) * 512 + 128:(sb0 + 1) * 512 + 129], lhsT=ones_f[0:1, :], rhs=sinkT[0:1, hh_:hh_ + 1],
                                    start=False, stop=True, skip_group_check=True)
                S.op("pe", smm, reads=[f"qg{gs}", "kS", "ident_b", "masks", "ones_f", "sinkT"], writes=[("ps", sb0), ("ps", sb0 + 1)])

            def st_B(i):
                sb0 = 2 * (i % 2)
                sps = ps_t[:, sb0 * 512:sb0 * 512 + 641]
                ss = i % 4
                st_ = stat[:, ss, :]
                S.op("dve", lambda e: e.reduce_max(out=st_[:, 0:1], in_=sps, axis=AX.X, negate=True),
                     reads=[("ps", sb0), ("ps", sb0 + 1)], writes=[("stat", ss, 0)])

            def st_C(i):
                sb0 = 2 * (i % 2)
                sl = i % 2
                sps = ps_t[:, sb0 * 512:sb0 * 512 + 641]
                ss = i % 4
                st_ = stat[:, ss, :]
                S.op("act", lambda e: e.activation(out=pf[sl][:, 0:641], in_=sps, func=AF.Exp, bias=st_[:, 0:1], scale=1.0, accum_out=st_[:, 1:2]),
                     reads=[("ps", sb0), ("ps", sb0 + 1), ("stat", ss, 0)], writes=[f"pf{sl}", ("stat", ss, 1)])

            def st_D(i):
                sl = i % 2
                ss = i % 4
                st_ = stat[:, ss, :]
                S.op("dve", lambda e: e.reciprocal(out=st_[:, 2:3], in_=st_[:, 1:2]),
                     reads=[("stat", ss, 1)], writes=[("stat", ss, 2)])
                S.op("pool", lambda e: e.tensor_scalar(out=pbf[sl], in0=pf[sl][:, 0:640], scalar1=st_[:, 2:3], scalar2=0.0, op0=ALU.mult, op1=ALU.add),
                     reads=[f"pf{sl}", ("stat", ss, 2)], writes=[f"pbf{sl}"])

            def st_E(i):
                g, n, hh = heads[i]
                sl = i % 2

                pbk = PT_BANK + i % 3
                ptv = ptH(pbk)

                def ptr(e):
                    for kt in range(5):
                        inst = e.transpose(ptv[:, kt, :], pbf[sl][:, kt * 128:(kt + 1) * 128], ident_b)
                    return inst
                S.op("pe", ptr, reads=[f"pbf{sl}", "ident_b"], writes=[("ps", pbk)])

            def st_V(i):
                g, n, hh = heads[i]
                pbk = PT_BANK + i % 3
                pk = (i // 4) % 2
                S.op("act", lambda e: e.activation(out=ptSb[pk][:, :, hh * 128:(hh + 1) * 128], in_=ptH(pbk), func=AF.Copy),
                     reads=[("ps", pbk)], writes=[(f"ptS{pk}", hh)])

            def st_F(i):
                g, n, hh = heads[i]
                kb, mk = keyblocks(n)
                pk = (i // 4) % 2

                def pv(e):
                    tiles = kb + [16, 17]
                    for kt in range(5):
                        inst = e.matmul(bank(OB), lhsT=vS[:, tiles[kt], g * 128:(g + 1) * 128], rhs=ptSb[pk][:, kt, :], start=(kt == 0), stop=(kt == 4))
                    return inst
                S.op("pe", pv, reads=["vS"] + [(f"ptS{pk}", h_) for h_ in range(4)], writes=[("ps", OB)])
                dsto = oT[:, g * 4:(g + 1) * 4, n * 128:(n + 1) * 128]
                pend["o"] = lambda: S.op("dve", lambda e: e.tensor_copy(out=dsto, in_=bank(OB).rearrange("p (h q) -> p h q", h=4)),
                                         reads=[("ps", OB)], writes=[("inT", g, n)])

            pend = {"o": None}
            load_q(0)
            st_A(0)
            st_A(1)
            st_B(0)
            for i in range(NHD):
                st_C(i)
                if pend["o"] is not None:
                    pend["o"]()
                    pend["o"] = None
                if i + 2 < NHD:
                    st_A(i + 2)
                if i + 1 < NHD:
                    st_B(i + 1)
                st_D(i)
                st_E(i)
                if i >= 3 and (i - 3) % 4 == 3:
                    st_F(i - 3)
                if i >= 2:
                    st_V(i - 2)
            st_V(NHD - 2)
            st_V(NHD - 1)
            st_F(NHD - 1)
            pend["o"]()
            if dbg_o is not None:
                S.op("sp", lambda e: e.dma_start(out=dbg_o.rearrange("(kc p) t -> p kc t", p=128), in_=oT[:, :, 0:SL]),
                     reads=[("inT", g, n) for g in range(4) for n in range(16)], writes=["dbg_o"], dma_sem="dbg")
            S.free("kS", "vS", "qg0", "qg1", "pf0", "pf1", "pbf0", "pbf1", "ptS0", "ptS1", "stat", "ones_f")
        if active("l1_wo"):
            tbs3 = [(t0, t0, n) for (t0, n) in tblocks(SL)]
            gemm_resid(1, wo_d, KC, 512, 4, oT, lambda r0, n: [("inT", g, nn) for g in range(4) for nn in range(r0 // 128, (r0 + n) // 128)],
                       tbs3, 2, TMP2_OFF)
        if active("l1_attn"):
            S.free("inT")
        if active("l1_norm2"):
            mlp(1, SL)

        if active("final"):
            fst = S.carve("fst", IN_OFF, [KC, 256], F32)
            osb = [S.carve(f"osb{i}", IN_OFF + 16384 + i * 8192, [D], F32) for i in range(2)]
            oi = {"i": 0}

            def final_tail(sb):
                for half in range(2):
                    tt = sb * 2 + half
                    o = oi["i"] % 2
                    oi["i"] += 1
                    for q in range(4):
                        pb = next_bank(6)

                        def tr(e, q=q, pb=pb, half=half):
                            for j in range(4):
                                inst = e.transpose(bank(pb)[:, j * 128:(j + 1) * 128], fst[:, 4 * q + j, half * 128:(half + 1) * 128], ident_f)
                            return inst
                        S.op("pe", tr, reads=[("fst", kc) for kc in range(4 * q, 4 * q + 4)] + ["ident_f"], writes=[("ps", pb)])
                        if q % 2 == 0:
                            S.op("dve", lambda e, o=o, q=q, pb=pb: e.tensor_copy(out=osb[o][:, q * 512:(q + 1) * 512], in_=bank(pb)),
                                 reads=[("ps", pb)], writes=[(f"osb{o}", q)])
                        else:
                            S.op("act", lambda e, o=o, q=q, pb=pb: e.activation(out=osb[o][:, q * 512:(q + 1) * 512], in_=bank(pb), func=AF.Copy),
                                 reads=[("ps", pb)], writes=[(f"osb{o}", q)])
                    S.op("sp", lambda e, o=o, tt=tt: e.dma_start(out=y_d[tt * 128:(tt + 1) * 128, :], in_=osb[o]),
                         reads=[(f"osb{o}", q) for q in range(4)], writes=[("y", tt)], dma_sem=f"osb{o}")

            sqf = S.carve("sqf", IN_OFF + 32768, [KC, 256], BF16)
            norm_phase(1, 0, SL, lambda kc, t0: fst[:, kc, :], lambda kc, t0: ("fst", kc), final=True,
                       sq_fn=lambda sb: (sqf, ["sqf"]))
            S.wait_all("sp", ["osb0", "osb1"])
        else:
            S.wait_all("sp", [s for s in S.cnt if not s.startswith("e_")])
        S.emit()
    return nc


def _fm(v):
    v = np.asarray(v, np.float32)
    return np.ascontiguousarray(v.reshape(-1, 128).T)


def _rope_tables():
    rows_n = SL // 64
    row = np.repeat(np.arange(rows_n), 64).astype(np.float32)
    col = np.tile(np.arange(64), rows_n).astype(np.float32)
    nf = 32
    inv_freq = (np.float32(10000.0) ** (-np.arange(nf, dtype=np.float32) / np.float32(nf))).astype(np.float32)
    ang_r = row[:, None] * inv_freq[None, :]
    ang_c = col[:, None] * inv_freq[None, :]
    ang = np.concatenate([ang_r, ang_r, ang_c, ang_c], axis=-1).astype(np.float32)
    return np.ascontiguousarray(np.cos(ang).T.astype(np.float32)), np.ascontiguousarray(np.sin(ang).T.astype(np.float32))


def _consts():
    cosT, sinT = _rope_tables()
    R = np.zeros((128, 128), np.float32)
    for d_out in range(128):
        half = (d_out // 32) % 2
        if half == 0:
            R[d_out + 32, d_out] = -1.0
        else:
            R[d_out - 32, d_out] = 1.0
    ii = np.arange(128)[:, None]
    jj = np.arange(128)[None, :]
    m_prev = np.where(jj >= ii, 0.0, MASKV)
    m_next = np.where(jj <= ii, 0.0, MASKV)
    m_full = np.full((128, 128), MASKV)
    masks = np.stack([m_prev, m_next, m_full], axis=1).astype(np.float32).reshape(128, 384)
    return cosT, sinT, R, masks, np.eye(128, dtype=np.float32)


def make_in_maps(inputs, cores):
    f = {k: np.asarray(v) for k, v in inputs.items()}
    cosT, sinT, R, masks, ident = _consts()
    shared = {
        "mod_w": np.ascontiguousarray(f["mod_w"], np.float32),
        "mod_bT": np.ascontiguousarray(np.concatenate([_fm(f["mod_b"][0]), _fm(f["mod_b"][1])], axis=1)),
        "n1T": np.ascontiguousarray(np.concatenate([_fm(f["norm1_g"][0]), _fm(f["norm1_g"][1])], axis=1)),
        "n2T": np.ascontiguousarray(np.concatenate([_fm(f["norm2_g"][0]), _fm(f["norm2_g"][1])], axis=1)),
        "fgT": _fm(f["final_g"]),
        "conv_w_in": np.ascontiguousarray(f["conv_w_in"][0], np.float32),
        "conv_wT": np.ascontiguousarray(np.stack([_fm(f["conv_w"][0, t]) for t in range(3)], axis=2).reshape(128, KC * 3)),
        "conv_w_out": np.ascontiguousarray(f["conv_w_out"][0], np.float32),
        "w_qkv": np.ascontiguousarray(f["attn_w_qkv"][0], np.float32),
        "sink": np.ascontiguousarray(np.broadcast_to(f["attn_sink"][0].astype(np.float32)[None, :], (128, NH))),
        "w_o": np.ascontiguousarray(f["attn_w_o"][0], np.float32),
        "w1": np.ascontiguousarray(f["mlp_w1"], np.float32),
        "w2": np.ascontiguousarray(f["mlp_w2"], np.float32),
        "cosT": cosT, "sinT": sinT, "rotR": R, "masks": masks, "ident": ident,
    }
    maps = []
    cc = _fm(f["c_ctx"])
    for b in cores:
        m = dict(shared)
        m["x"] = np.ascontiguousarray(f["x"][b], np.float32)
        m["ctx"] = np.ascontiguousarray(f["ctx"][b], np.float32)
        m["cT"] = np.ascontiguousarray(np.stack([_fm(f["c"][b]), cc], axis=2).reshape(128, KC * 2))
        maps.append(m)
    return maps


_NC_CACHE = {}


def kernel(**inputs):
    if "nc" not in _NC_CACHE:
        _NC_CACHE["nc"] = build()
    nc = _NC_CACHE["nc"]
    in_maps = make_in_maps(inputs, list(range(8)))
    res = run_bass_kernel_spmd(nc, in_maps, core_ids=list(range(8)))
    return np.stack([np.asarray(r["y"], np.float32) for r in res.results], axis=0)
```

```python
import contextlib
import math

import numpy as np
import concourse.bass as bass
import concourse.mybir as mybir
from concourse.bass_utils import run_bass_kernel_spmd

F32 = mybir.dt.float32
BF16 = mybir.dt.bfloat16
U8 = mybir.dt.uint8
AF = mybir.ActivationFunctionType
ALU = mybir.AluOpType
AX = mybir.AxisListType

D = 2048
SL = 2048
LC = 256
T = SL + LC
KC = D // 128
DFF = 8192
NH = 16
NKV = 4
EPS = 1e-6
SCALE = 1.0 / math.sqrt(128.0)
MASKV = -30000.0

ENG = ("pe", "act", "dve", "pool", "sp")
EPOCH = 8000
ARENA = 212800


def _merge(dst, src):
    for s, v in src.items():
        if dst.get(s, 0) < v:
            dst[s] = v


class Sch:
    def __init__(self, nc, stack):
        self.nc = nc
        self.stack = stack
        self.ops = {e: [] for e in ENG}
        self.sem_handles = {}
        self.cnt = {}
        self.seen = {e: {} for e in ENG}
        self.res = {}
        self.base = {}
        self.eng_epoch = {e: 0 for e in ENG}
        self.eng_cnt = {e: 0 for e in ENG}
        self.live = {}
        self.ghosts = []

    def sem(self, name):
        if name not in self.sem_handles:
            self.sem_handles[name] = self.stack.enter_context(self.nc.semaphore(name))
            self.cnt[name] = 0
        return self.sem_handles[name]

    @staticmethod
    def _root(key):
        return key[0] if isinstance(key, tuple) else key

    def _deps(self, reads, writes):
        deps = {}
        for r in reads:
            st = self.res.get(r)
            if st and st["w"]:
                _merge(deps, dict([st["w"]]))
        for w in writes:
            st = self.res.get(w)
            if st:
                if st["w"]:
                    _merge(deps, dict([st["w"]]))
                _merge(deps, st["r"])
            else:
                b = self.base.get(self._root(w))
                if b:
                    _merge(deps, b)
        return deps

    def op(self, eng, fn, reads=(), writes=(), dma_sem=None, accum=False):
        if eng == "pe" and not accum:
            for w in writes:
                st = self.res.get(w)
                assert not (st and st["w"] and st["w"][0].startswith("e_pe_") and not st["r"]), f"PE overwrites unread PE result {w}"
        if dma_sem is not None:
            for w in writes:
                st = self.res.get(w)
                if isinstance(w, str) and st and st["w"] and not st["r"]:
                    raise AssertionError(f"DMA overwrites unread buffer {w}")
        deps = self._deps(reads, writes)
        if dma_sem is None:
            if self.eng_cnt[eng] >= EPOCH:
                self.eng_epoch[eng] += 1
                self.eng_cnt[eng] = 0
            sname = f"e_{eng}_{self.eng_epoch[eng]}"
            self.sem(sname)
            self.eng_cnt[eng] += 1
            self.cnt[sname] += 1
            amount = 1
        else:
            sname = dma_sem
            self.sem(sname)
            self.cnt[sname] += 16
            amount = 16
        tok = (sname, self.cnt[sname])
        waits = []
        seen = self.seen[eng]
        for s, v in deps.items():
            if eng == "pe" and s.startswith("e_pe_"):
                continue
            if seen.get(s, 0) >= v:
                continue
            seen[s] = v
            waits.append((s, v))
        self.ops[eng].append((fn, waits, sname, amount))
        for w in writes:
            self.res[w] = {"w": tok, "r": {}}
        for r in reads:
            st = self.res.setdefault(r, {"w": None, "r": {}})
            if st["r"].get(tok[0], 0) < tok[1]:
                st["r"][tok[0]] = tok[1]
        return tok

    def wait_all(self, eng, sems):
        self.ops[eng].append((None, [(s, self.cnt[s]) for s in sems if self.cnt.get(s, 0) > 0], None, 0))

    def set_arena(self, tensor):
        self.arena = tensor

    def carve(self, name, off, shape, dtype):
        dsz = {F32: 4, BF16: 2, U8: 1}[dtype]
        n = 1
        for s in shape:
            n *= s
        nbytes = n * dsz
        assert off % 4 == 0 and off + nbytes <= ARENA, (name, off, nbytes)
        for ln, (lo, ls) in self.live.items():
            assert off + nbytes <= lo or lo + ls <= off, f"carve {name} overlaps live {ln}"
        inh = {}
        keep = []
        for (go, gs, gt) in self.ghosts:
            if off + nbytes <= go or go + gs <= off:
                keep.append((go, gs, gt))
                continue
            _merge(inh, gt)
            if not (off <= go and go + gs <= off + nbytes):
                keep.append((go, gs, gt))
        self.ghosts = keep
        self.base[name] = inh
        self.live[name] = (off, nbytes)
        ap = self.arena[:, off:off + nbytes].bitcast(dtype)
        if len(shape) == 2:
            ap = ap.rearrange("p (a b) -> p a b", a=shape[0], b=shape[1])
        elif len(shape) == 3:
            ap = ap.rearrange("p (a b c) -> p a b c", a=shape[0], b=shape[1], c=shape[2])
        return ap

    def free(self, *names):
        for name in names:
            off, nbytes = self.live.pop(name)
            toks = dict(self.base.pop(name, {}))
            for k in [k for k in self.res if self._root(k) == name]:
                st = self.res.pop(k)
                if st["w"]:
                    _merge(toks, dict([st["w"]]))
                _merge(toks, st["r"])
            self.ghosts.append((off, nbytes, toks))

    def emit(self):
        nc = self.nc
        with nc.Block() as block:
            def body(e):
                def run(engine):
                    for fn, waits, sname, amount in self.ops[e]:
                        for s, v in waits:
                            engine.wait_ge(self.sem_handles[s], v)
                        if fn is None:
                            continue
                        inst = fn(engine)
                        inst.then_inc(self.sem_handles[sname], amount)
                return run

            block.tensor(body("pe"))
            block.scalar(body("act"))
            block.vector(body("dve"))
            block.gpsimd(body("pool"))
            block.sync(body("sp"))


P_OFF = 0
W_OFF = 6144
W_SZ = 65536
IN_OFF = W_OFF + W_SZ
IN_SZ = 98304
TMP_OFF = IN_OFF + IN_SZ
TMP2_OFF = IN_OFF + 73728
MOD_OFF = ARENA - 16384

PHASES = ["p1", "mod", "l0_norm1", "l0_conv", "l0_wout", "l0_norm2", "l0_mlp1", "l0_mlp2",
          "l1_norm1", "l1_qkv", "l1_attn", "l1_wo", "l1_norm2", "l1_mlp1", "l1_mlp2", "final"]


def hres(oc, t0, n):
    return [("hT", oc, j) for j in range(t0 // 256, (t0 + n) // 256)]


def build(dbg_stop=None, dbg_outs=()):
    nc = bass.Bass("TRN2", target_bir_lowering=False)

    def din(name, shape, dt=F32):
        return nc.dram_tensor(name, list(shape), dt, kind="ExternalInput").ap()

    def dscr(name, shape, dt):
        kind = "ExternalOutput" if name in dbg_outs else "Internal"
        return nc.dram_tensor(name, list(shape), dt, kind=kind).ap()

    x_d = din("x", [SL, D])
    ctx_d = din("ctx", [LC, D])
    cT_d = din("cT", [128, KC * 2])
    modw_d = din("mod_w", [2, D, 6 * D])
    mbT_d = din("mod_bT", [128, 2 * 96])
    n1T_d = din("n1T", [128, 2 * KC])
    n2T_d = din("n2T", [128, 2 * KC])
    fgT_d = din("fgT", [128, KC])
    win_d = din("conv_w_in", [D, 3 * D])
    cwT_d = din("conv_wT", [128, KC * 3])
    wout_d = din("conv_w_out", [D, D])
    wqkv_d = din("w_qkv", [D, 3072])
    sink_d = din("sink", [128, NH])
    wo_d = din("w_o", [D, D])
    w1_d = din("w1", [2, D, DFF])
    w2_d = din("w2", [2, DFF, D])
    cos_d = din("cosT", [128, SL])
    sin_d = din("sinT", [128, SL])
    rot_d = din("rotR", [128, 128])
    mask_d = din("masks", [128, 3 * 128])
    id_d = din("ident", [128, 128])
    y_d = nc.dram_tensor("y", [SL, D], F32, kind="ExternalOutput").ap()

    hT = dscr("hT", [D, T], F32)
    vT = dscr("vT", [D, T], BF16)
    hidT = dscr("hidT", [DFF, T], BF16)
    qT = dscr("qT", [D, SL], BF16)
    kT = dscr("kT", [NKV * 128, T], BF16)
    vtok = dscr("vtok", [T, NKV * 128], BF16)
    dbg_u = dscr("dbg_u", [D, T], BF16) if "dbg_u" in dbg_outs else None
    dbg_mod = dscr("dbg_mod", [128, 4 * 96], F32) if "dbg_mod" in dbg_outs else None
    dbg_o = dscr("dbg_o", [D, SL], BF16) if "dbg_o" in dbg_outs else None

    stop_idx = PHASES.index(dbg_stop) if dbg_stop else len(PHASES) - 1

    def active(ph):
        return PHASES.index(ph) <= stop_idx

    with contextlib.ExitStack() as st:
        S = Sch(nc, st)
        arena_t = st.enter_context(nc.sbuf_tensor("arena", [128, ARENA], U8))
        S.set_arena(arena_t)
        ps_t = st.enter_context(nc.psum_tensor("ps", [128, 4096], F32))

        def bank(i, n=512):
            return ps_t[:, i * 512:i * 512 + n]

        ident_f = S.carve("ident_f", 0, [128], F32)
        rotR = S.carve("rotR", 512, [128], F32)
        ident_b = S.carve("ident_b", 1024, [128], BF16)
        ones_b = S.carve("ones_b", 1280, [128], BF16)
        masks = S.carve("masks", 1536, [3, 128], BF16)
        cT = S.carve("cT", 2304, [KC, 2], F32)
        scT = S.carve("scT", 2432, [KC, 2], BF16)
        mbT = S.carve("mbT", 2496, [2, 96], F32)
        modT = S.carve("modT", 3264, [2, 2, 96], F32)
        n1T = S.carve("n1T", 4800, [2, KC], F32)
        n2T = S.carve("n2T", 4928, [2, KC], F32)
        fgT = S.carve("fgT", 5056, [KC], F32)
        Amod = S.carve("Amod", 5120, [4, 2, KC], F32)
        cwT = S.carve("cwT", 5632, [KC, 3], F32)
        sinkT = S.carve("sinkT", 5824, [NH], F32)
        nsinkT = S.carve("nsinkT", 5888, [NH], F32)
        zeroT = S.carve("zeroT", 5952, [KC], F32)

        def sp_load(dst, src, res, sem):
            S.op("sp", lambda e: e.dma_start(out=dst, in_=src), writes=[res], dma_sem=sem)

        sp_load(ident_f, id_d, "ident_f", "c0")
        sp_load(rotR, rot_d, "rotR", "c1")
        sp_load(cT, cT_d.rearrange("p (k two) -> p k two", two=2), "cT", "c2")
        sp_load(mbT, mbT_d.rearrange("p (l j) -> p l j", l=2), "mbT", "c3")
        sp_load(n1T, n1T_d.rearrange("p (l k) -> p l k", l=2), "n1T", "c4")
        sp_load(n2T, n2T_d.rearrange("p (l k) -> p l k", l=2), "n2T", "c5")
        sp_load(fgT, fgT_d, "fgT", "c6")
        sp_load(cwT, cwT_d.rearrange("p (k t) -> p k t", t=3), "cwT", "c7")
        sp_load(sinkT, sink_d, "sinkT", "c8")
        S.op("pool", lambda e: e.dma_start(out=ident_b, in_=id_d), writes=["ident_b"], dma_sem="c9")
        S.op("pool", lambda e: e.dma_start(out=masks, in_=mask_d.rearrange("p (m k) -> p m k", m=3)),
             writes=["masks"], dma_sem="c10")
        S.op("dve", lambda e: e.memset(ones_b, 1.0), writes=["ones_b"])
        S.op("dve", lambda e: e.memset(zeroT, 0.0), writes=["zeroT"])
        S.op("dve", lambda e: e.tensor_scalar(out=nsinkT, in0=sinkT, scalar1=-1.0, scalar2=None, op0=ALU.mult),
             reads=["sinkT"], writes=["nsinkT"])
        S.op("act", lambda e: e.activation(out=scT, in_=cT, func=AF.Silu), reads=["cT"], writes=["scT"])

        wslot_state = {"n": 0}

        def load_slab(w_ap, col0, ncols, kcn, nslots, extra_reads=()):
            i = wslot_state["n"]
            wslot_state["n"] += 1
            slot = i % nslots
            nb = kcn * ncols * 2
            name = f"ws{i}"
            for ln in [ln for ln in S.live if ln.startswith("ws") and ln[2:].isdigit()]:
                lo, ls = S.live[ln]
                o = W_OFF + slot * nb
                if not (o + nb <= lo or lo + ls <= o):
                    S.free(ln)
            view = S.carve(name, W_OFF + slot * nb, [kcn, ncols], BF16)
            src = w_ap[:, col0:col0 + ncols].rearrange("(kc p) n -> p kc n", p=128)
            S.op("pool", lambda e: e.dma_start(out=view, in_=src), reads=list(extra_reads), writes=[name], dma_sem=f"w{slot}_{nb}")
            return name, view

        def free_slabs():
            for ln in [ln for ln in S.live if ln.startswith("ws") and ln[2:].isdigit()]:
                S.free(ln)
            wslot_state["n"] = 0

        psring = {"i": 0}

        def next_bank(nb):
            b = psring["i"] % nb
            psring["i"] += 1
            return b

        def mm_group(pb, wname, wview, c0, in_view, kcn, t0, n, in_res, extra_reads=()):
            def fn(e):
                for kc in range(kcn):
                    inst = e.matmul(bank(pb, n), lhsT=wview[:, kc, c0:c0 + 128], rhs=in_view[:, kc, t0:t0 + n],
                                    start=(kc == 0), stop=(kc == kcn - 1))
                return inst
            S.op("pe", fn, reads=[wname] + list(in_res) + list(extra_reads), writes=[("ps", pb)])

        ms = [S.carve(f"ms{i}", MOD_OFF + i * 8192, [KC, 256], BF16) for i in range(2)]
        mstate = {"n": 0, "slot": {}}

        def mod_dma(l, s_, slots=None):
            i = mstate["n"]
            mstate["n"] += 1
            if slots is None:
                name, view = f"ms{i % 2}", ms[i % 2]
            else:
                name, view = slots[i % len(slots)]
            src = modw_d[l][:, s_ * 256:(s_ + 1) * 256].rearrange("(kc p) n -> p kc n", p=128)
            S.op("pool", lambda e: e.dma_start(out=view, in_=src), writes=[name], dma_sem="d_" + name)
            mstate["slot"][(l, s_)] = (name, view)

        def mod_pe(l, s_):
            name, view = mstate["slot"].pop((l, s_))
            mpb = 6 + l

            def fn(e):
                for c in range(2):
                    j = s_ * 2 + c
                    for kc in range(KC):
                        inst = e.matmul(bank(mpb)[:, 2 * j:2 * j + 2], lhsT=view[:, kc, c * 128:(c + 1) * 128],
                                        rhs=scT[:, kc, :], start=(kc == 0), stop=(kc == KC - 1), skip_group_check=True)
                return inst
            S.op("pe", fn, reads=[name, "scT"], writes=[("ps", mpb)], accum=True)

        def mod_finalize(l, part):
            mpb = 6 + l
            j0, j1 = (0, 32) if part == 0 else (32, 96)
            for kind in range(2):
                srcp = bank(mpb)[:, 0:192].rearrange("p (j two) -> p j two", two=2)[:, j0:j1, kind]
                dst = modT[:, l, kind, j0:j1]
                S.op("dve", lambda e, dst=dst, srcp=srcp: e.tensor_tensor(out=dst, in0=srcp, in1=mbT[:, l, j0:j1], op=ALU.add),
                     reads=[("ps", mpb), "mbT"], writes=[("modT", l, kind, part)])
                which = part
                nT = n1T if which == 0 else n2T
                s_ap = modT[:, l, kind, 16 + 48 * which:32 + 48 * which]
                dstA = Amod[:, l * 2 + kind, which, :]
                S.op("dve", lambda e, dstA=dstA, s_ap=s_ap, nT=nT: e.scalar_tensor_tensor(
                    out=dstA, in0=s_ap, scalar=1.0, in1=nT[:, l, :], op0=ALU.add, op1=ALU.mult),
                    reads=[("modT", l, kind, part), "n1T", "n2T"], writes=[("Amod", l, kind, which)])

        def dump_mod():
            if dbg_mod is not None:
                S.op("sp", lambda e: e.dma_start(out=dbg_mod, in_=modT.rearrange("p l k j -> p (l k j)")),
                     reads=[("modT", l, k, p_) for l in range(2) for k in range(2) for p_ in range(2)], writes=["dbg_mod"], dma_sem="dbg")

        def mod_vec(l, kind, idx, kc):
            return modT[:, l, kind, idx * 16 + kc:idx * 16 + kc + 1]

        def mod_res(l, kind, idx):
            return ("modT", l, kind, 0 if idx < 2 else 1)

        def norm_phase(l, which, ntok, dst_fn, dst_res_fn, final=False, sq_fn=None, hb_offs=None, rs_off=None, pre_iter=None):
            base = TMP2_OFF
            if hb_offs is None:
                hb_offs = [base + i * 16384 for i in range(3)]
                rs_off = base + 49152
            hb = [S.carve(f"hb{i}", hb_offs[i], [KC, 256], F32) for i in range(3)]
            rs = [S.carve("rs0", rs_off, [256], F32)] * 2
            nsb = ntok // 256
            pbs = {}

            def stage0(sb):
                t0 = sb * 256
                sl = sb % 3
                S.op("sp", lambda e: e.dma_start(out=hb[sl], in_=hT[:, t0:t0 + 256].rearrange("(kc p) t -> p kc t", p=128)),
                     reads=[r for oc in range(KC) for r in hres(oc, t0, 256)], writes=[f"hb{sl}"], dma_sem=f"hb{sl}")

            def stage1(sb):
                sl = sb % 3
                sqv, sqres = sq_fn(sb)
                S.op("act", lambda e: e.activation(out=sqv, in_=hb[sl], func=AF.Square), reads=[f"hb{sl}"], writes=sqres)
                if final:
                    pb = 6 + sb % 2
                elif pre_iter is not None:
                    pb = 4 + sb % 2
                else:
                    pb = next_bank(6)
                pbs[sb] = pb

                def ssq(e):
                    for kc in range(KC):
                        inst = e.matmul(bank(pb, 256), lhsT=ones_b, rhs=sqv[:, kc, :], start=(kc == 0), stop=(kc == KC - 1))
                    return inst
                S.op("pe", ssq, reads=sqres + ["ones_b"], writes=[("ps", pb)])

            def stage2(sb):
                t0 = sb * 256
                kind = 1 if t0 >= SL else 0
                sl = sb % 3
                r2 = 0
                pb = pbs[sb]
                S.op("dve", lambda e: e.tensor_scalar(out=rs[r2], in0=bank(pb, 256), scalar1=1.0 / D, scalar2=EPS, op0=ALU.mult, op1=ALU.add),
                     reads=[("ps", pb)], writes=[f"rs{r2}"])
                S.op("act", lambda e: e.activation(out=rs[r2], in_=rs[r2], func=AF.Sqrt), reads=[f"rs{r2}"], writes=[f"rs{r2}"])
                S.op("dve", lambda e: e.reciprocal(out=rs[r2], in_=rs[r2]), reads=[f"rs{r2}"], writes=[f"rs{r2}"])
                S.op("dve", lambda e: e.tensor_tensor(out=hb[sl], in0=hb[sl], in1=rs[r2].unsqueeze(1).broadcast_to([128, KC, 256]), op=ALU.mult),
                     reads=[f"hb{sl}", f"rs{r2}"], writes=[f"hb{sl}"])
                for kc in range(KC):
                    if final:
                        a_ap = fgT[:, kc:kc + 1]
                        b_ap = zeroT[:, kc:kc + 1]
                        mres = ["fgT", "zeroT"]
                    else:
                        a_ap = Amod[:, l * 2 + kind, which, kc:kc + 1]
                        b_ap = mod_vec(l, kind, 3 * which, kc)
                        mres = [("Amod", l, kind, which), mod_res(l, kind, 3 * which)]
                    dst = dst_fn(kc, t0)
                    if kc % 2 == 0:
                        S.op("act", lambda e, kc=kc, dst=dst, a_ap=a_ap, b_ap=b_ap: e.activation(out=dst, in_=hb[sl][:, kc, :], func=AF.Identity, bias=b_ap, scale=a_ap),
                             reads=[f"hb{sl}"] + mres, writes=[dst_res_fn(kc, t0)])
                    else:
                        S.op("dve", lambda e, kc=kc, dst=dst, a_ap=a_ap, b_ap=b_ap: e.tensor_scalar(out=dst, in0=hb[sl][:, kc, :], scalar1=a_ap, scalar2=b_ap, op0=ALU.mult, op1=ALU.add),
                             reads=[f"hb{sl}"] + mres, writes=[dst_res_fn(kc, t0)])
                if final:
                    final_tail(sb)

            for sb in range(min(2, nsb)):
                stage0(sb)
            stage1(0)
            for sb in range(nsb):
                if pre_iter is not None:
                    pre_iter(sb)
                if sb + 2 < nsb:
                    stage0(sb + 2)
                if sb + 1 < nsb:
                    stage1(sb + 1)
                stage2(sb)
            S.free("hb0", "hb1", "hb2", "rs0")

        def dump_u(inT, ntok):
            if dbg_u is not None:
                S.op("sp", lambda e: e.dma_start(out=dbg_u[:, 0:ntok].rearrange("(kc p) t -> p kc t", p=128), in_=inT[:, :, 0:ntok]),
                     reads=in_res(0, ntok), writes=["dbg_u"], dma_sem="dbg")

        def std_norm(l, which, ntok, hb_offs=None, rs_off=None, pre_iter=None):
            inT = S.carve("inT", IN_OFF, [KC, T], BF16)
            norm_phase(l, which, ntok, lambda kc, t0: inT[:, kc, t0:t0 + 256], lambda kc, t0: ("inT", t0 // 256, kc),
                       sq_fn=lambda sb: (inT[:, :, sb * 256:(sb + 1) * 256], [("inT", sb, kc) for kc in range(KC)]),
                       hb_offs=hb_offs, rs_off=rs_off, pre_iter=pre_iter)
            return inT

        def in_res(t0, n):
            return [("inT", j, kc) for j in range(t0 // 256, (t0 + n + 255) // 256) for kc in range(KC)]

        def tblocks(ntok):
            out = []
            t0 = 0
            while t0 < ntok:
                n = min(512, ntok - t0)
                out.append((t0, n))
                t0 += n
            return out

        def resid_setup(base):
            hold = [S.carve(f"hold{i}", base + i * 2048, [512], F32) for i in range(4)]
            hnew = [S.carve(f"hnew{i}", base + 8192 + i * 2048, [512], F32) for i in range(4)]
            return hold, hnew

        rs = {"i": 0}

        def resid_prefetch(hold, oc, t0, n):
            i = rs["i"]
            rs["i"] += 1
            sl = i % 4
            S.op("sp", lambda e: e.dma_start(out=hold[sl][:, 0:n], in_=hT[oc * 128:(oc + 1) * 128, t0:t0 + n]),
                 reads=hres(oc, t0, n), writes=[f"hold{sl}"], dma_sem=f"hold{sl}")
            return sl

        def resid_apply(hold, hnew, sl, pb, gate_ap, gate_res, oc, t0, n):
            S.op("dve", lambda e: e.scalar_tensor_tensor(out=hnew[sl][:, 0:n], in0=bank(pb, n), scalar=gate_ap, in1=hold[sl][:, 0:n],
                                                        op0=ALU.mult, op1=ALU.add),
                 reads=[("ps", pb), f"hold{sl}"] + gate_res, writes=[f"hnew{sl}"])
            S.op("sp", lambda e: e.dma_start(out=hT[oc * 128:(oc + 1) * 128, t0:t0 + n], in_=hnew[sl][:, 0:n]),
                 reads=[f"hnew{sl}"], writes=hres(oc, t0, n), dma_sem=f"hnew{sl}")

        def gemm_resid(l, w_ap, kcn, ncols_slab, nslots, inT, in_resf, tbs, gate_idx, tmpbase, slabs=None):
            hold, hnew = resid_setup(tmpbase)
            nslab = D // ncols_slab
            cps = ncols_slab // 128
            loaded = {}
            defer2 = (kcn == 64)
            for s in range(min(nslots, nslab)):
                if defer2 and s == 1:
                    continue
                loaded[s] = load_slab(w_ap, s * ncols_slab, ncols_slab, kcn, nslots)
            tiles = [(s, c, tb) for s in range(nslab) for c in range(cps) for tb in tbs]
            PF = 3
            pre = {}
            for i in range(min(PF, len(tiles))):
                s, c, (a0, r0, n) = tiles[i]
                pre[i] = resid_prefetch(hold, s * cps + c, a0, n)
            for i, (s, c, (a0, r0, n)) in enumerate(tiles):
                oc = s * cps + c
                wname, wview = loaded[s]
                pb = next_bank(6)
                mm_group(pb, wname, wview, c * 128, inT, kcn, r0, n, in_resf(r0, n))
                if defer2 and i == 0:
                    loaded[1] = load_slab(w_ap, ncols_slab, ncols_slab, kcn, nslots, extra_reads=[("ps", pb)])
                kind = 1 if a0 >= SL else 0
                resid_apply(hold, hnew, pre[i], pb, mod_vec(l, kind, gate_idx, oc), [mod_res(l, kind, gate_idx)], oc, a0, n)
                if i + PF < len(tiles):
                    s2, c2, (a2, r2, n2) = tiles[i + PF]
                    pre[i + PF] = resid_prefetch(hold, s2 * cps + c2, a2, n2)
                if c == cps - 1 and tb_is_last(tbs, a0) and s + nslots < nslab:
                    loaded[s + nslots] = load_slab(w_ap, (s + nslots) * ncols_slab, ncols_slab, kcn, nslots)
            free_slabs()
            S.free(*[f"hold{i}" for i in range(4)], *[f"hnew{i}" for i in range(4)])

        def tb_is_last(tbs, a0):
            return a0 == tbs[-1][0]

        def mlp(l, ntok):
            tbs = tblocks(ntok)
            E0 = TMP2_OFF + 34816
            rt = [S.carve(f"rt{i}", E0 + i * 2048, [512], F32) for i in range(2)]
            hs = [S.carve(f"hs{i}", E0 + 4096 + i * 4608, [T], BF16) for i in range(2)]
            hsA = [S.carve(f"hsA{i}", E0 + 13312 + i * 1024, [512], BF16) for i in range(2)]
            inT = std_norm(l, 1, ntok, hb_offs=[TMP2_OFF, TMP2_OFF + 16384, W_OFF + 49152], rs_off=TMP2_OFF + 32768)
            dump_u(inT, ntok) if dbg_stop == f"l{l}_norm2" else None
            if not active(f"l{l}_mlp1"):
                S.free("inT", "rt0", "rt1", "hs0", "hs1", "hsA0", "hsA1")
                return
            nslab = DFF // 512
            NS = 3
            loaded = {}
            for s in range(NS):
                loaded[s] = load_slab(w1_d[l], s * 512, 512, KC, NS)
            if l == 0:
                mod_dma(1, 0)
            ri = {"i": 0}

            def epi(pb, n, dst):
                r = ri["i"] % 2
                ri["i"] += 1
                S.op("act", lambda e: e.activation(out=rt[r][:, 0:n], in_=bank(pb, n), func=AF.Relu),
                     reads=[("ps", pb)], writes=[f"rt{r}"])
                return r

            def hid_groups(t0, n):
                return sorted({min(t // 768, 2) for t in (t0, t0 + n - 1)})

            tile_i = 0
            for ti, (t0, n) in enumerate(tbs):
                for ch in range(8):
                    wname, wview = loaded[ch // 4]
                    pb = next_bank(6)
                    mm_group(pb, wname, wview, (ch % 4) * 128, inT, KC, t0, n, in_res(t0, n))
                    r = epi(pb, n, None)
                    hsl = tile_i % 2
                    tile_i += 1
                    S.op("dve", lambda e, r=r, hsl=hsl, n=n: e.tensor_tensor(out=hsA[hsl][:, 0:n], in0=rt[r][:, 0:n], in1=rt[r][:, 0:n], op=ALU.mult),
                         reads=[f"rt{r}"], writes=[f"hsA{hsl}"])
                    S.op("sp", lambda e, hsl=hsl, ch=ch, t0=t0, n=n: e.dma_start(out=hidT[ch * 128:(ch + 1) * 128, t0:t0 + n], in_=hsA[hsl][:, 0:n]),
                         reads=[f"hsA{hsl}"], writes=[("hidT", ch, g) for g in hid_groups(t0, n)], dma_sem=f"hsA{hsl}")
                    if l == 0 and ti == len(tbs) - 1:
                        mod_pe(1, ch)
                        mod_dma(1, ch + 1)
            for s in (3, 4):
                loaded[s] = load_slab(w1_d[l], s * 512, 512, KC, NS)
            ci = 0
            for s in range(2, nslab):
                wname, wview = loaded[s]
                for c in range(4):
                    ch = s * 4 + c
                    hsl = ci % 2
                    ci += 1
                    for ti, (t0, n) in enumerate(tbs):
                        pb = next_bank(6)
                        mm_group(pb, wname, wview, c * 128, inT, KC, t0, n, in_res(t0, n))
                        r = epi(pb, n, None)
                        S.op("dve", lambda e, r=r, hsl=hsl, t0=t0, n=n: e.tensor_tensor(out=hs[hsl][:, t0:t0 + n], in0=rt[r][:, 0:n], in1=rt[r][:, 0:n], op=ALU.mult),
                             reads=[f"rt{r}"], writes=[(f"hs{hsl}", ti)])
                    if l == 0 and ch < 48:
                        mod_pe(1, ch)
                        if ch + 1 < 48:
                            mod_dma(1, ch + 1)
                    S.op("sp", lambda e, hsl=hsl, ch=ch: e.dma_start(out=hidT[ch * 128:(ch + 1) * 128, 0:ntok], in_=hs[hsl][:, 0:ntok]),
                         reads=[(f"hs{hsl}", ti) for ti in range(len(tbs))], writes=[("hidT", ch, g) for g in range(3)], dma_sem=f"hs{hsl}")
                if s >= 2 and s + NS < nslab:
                    loaded[s + NS] = load_slab(w1_d[l], (s + NS) * 512, 512, KC, NS)
            if l == 0:
                mod_finalize(1, 0)
                mod_finalize(1, 1)
                dump_mod()
            free_slabs()
            S.free("rt0", "rt1", "hs0", "hs1", "hsA0", "hsA1", "inT")
            if not active(f"l{l}_mlp2"):
                return
            groups = [(0, 768), (768, 768), (1536, ntok - 1536)]
            for gi, (g0, gn) in enumerate(groups):
                inH = S.carve("inH", IN_OFF, [64, 768], BF16)
                for qk in range(4):
                    S.op("sp", lambda e, qk=qk, g0=g0, gn=gn: e.dma_start(
                        out=inH[:, qk * 16:(qk + 1) * 16, 0:gn],
                        in_=hidT[qk * 2048:(qk + 1) * 2048, g0:g0 + gn].rearrange("(kc p) t -> p kc t", p=128)),
                        reads=[("hidT", ch, gi) for ch in range(qk * 16, (qk + 1) * 16)], writes=[("inH", qk)], dma_sem=f"inH{qk}")
                tbs2 = [(g0 + r0, r0, n) for (r0, n) in tblocks(gn)]
                gemm_resid(l, w2_d[l], 64, 256, 2, inH, lambda r0, n: [("inH", qk) for qk in range(4)], tbs2, 5, TMP_OFF)
                S.free("inH")

        pre_slabs = {}
        if active("p1"):
            msw = [(f"msw{i}", S.carve(f"msw{i}", W_OFF + i * 8192, [KC, 256], BF16)) for i in range(4)]
            xs = [S.carve(f"xs{i}", W_OFF + 32768 + i * 8192, [D], F32) for i in range(2)]
            hst = [S.carve(f"hst{i}", W_OFF + 49152 + i * 8192, [KC, 128], F32) for i in range(2)]
            with_mod = active("mod")
            if with_mod:
                for s_ in range(4):
                    mod_dma(0, s_, slots=msw)
            tstate = {"n": 0}

            def p1_tile(i):
                src = x_d[i * 128:(i + 1) * 128, :] if i < 16 else ctx_d[(i - 16) * 128:(i - 15) * 128, :]
                sl = i % 2
                S.op("sp", lambda e: e.dma_start(out=xs[sl], in_=src), writes=[f"xs{sl}"], dma_sem=f"xs{sl}")
                for q in range(4):
                    pb = q

                    def tr(e, q=q, pb=pb):
                        for j in range(4):
                            inst = e.transpose(bank(pb)[:, j * 128:(j + 1) * 128], xs[sl][:, (4 * q + j) * 128:(4 * q + j + 1) * 128], ident_f)
                        return inst
                    S.op("pe", tr, reads=[f"xs{sl}", "ident_f"], writes=[("ps", pb)])
                    dst = hst[sl][:, 4 * q:4 * q + 4, :]
                    srcp = bank(pb).rearrange("p (a b) -> p a b", a=4)
                    if q % 2 == 0:
                        S.op("dve", lambda e, dst=dst, srcp=srcp: e.tensor_copy(out=dst, in_=srcp),
                             reads=[("ps", pb)], writes=[(f"hst{sl}", q)])
                    else:
                        S.op("act", lambda e, dst=dst, srcp=srcp: e.activation(out=dst, in_=srcp, func=AF.Copy),
                             reads=[("ps", pb)], writes=[(f"hst{sl}", q)])
                if with_mod and i < 8:
                    for k in (2 * i, 2 * i + 1):
                        mod_pe(0, k)
                        if k + 4 < 16:
                            mod_dma(0, k + 4, slots=msw)
                t0 = i * 128
                S.op("sp", lambda e: e.dma_start(out=hT[:, t0:t0 + 128].rearrange("(kc p) t -> p kc t", p=128), in_=hst[sl]),
                     reads=[(f"hst{sl}", q) for q in range(4)],
                     writes=[r for oc in range(KC) for r in hres(oc, t0 - t0 % 256, 256)], dma_sem=f"hst{sl}")

            def emit_tiles(upto):
                while tstate["n"] <= min(upto, 17):
                    p1_tile(tstate["n"])
                    tstate["n"] += 1

            emit_tiles(7)
            if with_mod:
                mod_finalize(0, 0)
            S.free(*[f"msw{i}" for i in range(4)])
            if active("l0_conv"):
                pre_slabs[0] = load_slab(win_d, D, 512, KC, 4)
                pre_slabs[1] = load_slab(win_d, 2 * D, 512, KC, 4)
            if active("l0_norm1"):
                inT = std_norm(0, 0, T, pre_iter=lambda sb: emit_tiles(2 * sb + 5))
            emit_tiles(17)
            S.free("xs0", "xs1", "hst0", "hst1")
            if active("l0_norm1") and dbg_stop == "l0_norm1":
                dump_u(inT, T)

        if active("l0_conv"):
            tbs = tblocks(T)
            ctmp = [S.carve(f"ctmp{i}", TMP2_OFF + i * 2048, [512], F32) for i in range(2)]
            zb = [S.carve(f"zb{i}", TMP2_OFF + 4096 + i * 9248, [2312], F32) for i in range(2)]
            cvb = [S.carve(f"cvb{i}", TMP2_OFF + 22592 + i * 9248, [2312], F32) for i in range(2)]
            vst = [S.carve(f"vst{i}", TMP2_OFF + 41088 + i * 4608, [T], BF16) for i in range(2)]
            for i in range(2):
                for pi, (p0, p1) in enumerate(((0, 1), (2049, 2051), (2307, 2312))):
                    S.op("dve", lambda e, i=i, p0=p0, p1=p1: e.memset(zb[i][:, p0:p1], 0.0), writes=[(f"zb{i}", "pad", pi)])

            def zoff(t0):
                return 1 + t0 if t0 < SL else 2051 + (t0 - SL)

            def cvoff(t0):
                return t0 if t0 < SL else 2050 + (t0 - SL)

            slab_cols = []
            for g in range(4):
                slab_cols += [2 * D + g * 512 - D, 2 * D + g * 512, g * 512]
            loaded = {}
            for i in range(4):
                loaded[i] = pre_slabs[i] if i in pre_slabs else load_slab(win_d, slab_cols[i], 512, KC, 4)
            nload = 4
            mod_dma(0, 16)

            def b_part(f):
                g, fl = divmod(f, 4)
                zi = f % 2
                wname, wview = loaded[3 * g + 2]
                for ti, (t0, n) in enumerate(tbs):
                    pb = next_bank(6)
                    mm_group(pb, wname, wview, fl * 128, inT, KC, t0, n, in_res(t0, n))
                    S.op("dve", lambda e, pb=pb, zi=zi, t0=t0, n=n: e.tensor_tensor(
                        out=vst[zi][:, t0:t0 + n], in0=bank(pb, n), in1=cvb[zi][:, cvoff(t0):cvoff(t0) + n], op=ALU.mult),
                        reads=[("ps", pb), f"cvb{zi}"], writes=[(f"vst{zi}", ti)])
                S.op("sp", lambda e, zi=zi, f=f: e.dma_start(out=vT[f * 128:(f + 1) * 128, :], in_=vst[zi]),
                     reads=[(f"vst{zi}", ti) for ti in range(len(tbs))], writes=[("vT", f)], dma_sem=f"vst{zi}")

            for f in range(KC):
                g, fl = divmod(f, 4)
                zi = f % 2
                wcn, wcv = loaded[3 * g]
                whn, whv = loaded[3 * g + 1]
                for ti, (t0, n) in enumerate(tbs):
                    pa = next_bank(6)
                    mm_group(pa, wcn, wcv, fl * 128, inT, KC, t0, n, in_res(t0, n))
                    pbk = next_bank(6)
                    mm_group(pbk, whn, whv, fl * 128, inT, KC, t0, n, in_res(t0, n))
                    r = ti % 2
                    S.op("act", lambda e, r=r, pa=pa, n=n: e.activation(out=ctmp[r][:, 0:n], in_=bank(pa, n), func=AF.Copy),
                         reads=[("ps", pa)], writes=[f"ctmp{r}"])
                    S.op("dve", lambda e, r=r, pbk=pbk, zi=zi, t0=t0, n=n: e.tensor_tensor(
                        out=zb[zi][:, zoff(t0):zoff(t0) + n], in0=ctmp[r][:, 0:n], in1=bank(pbk, n), op=ALU.mult),
                        reads=[f"ctmp{r}", ("ps", pbk)], writes=[(f"zb{zi}", ti)])
                zres = [(f"zb{zi}", ti) for ti in range(len(tbs))] + [(f"zb{zi}", "pad", pi) for pi in range(3)] + ["cwT"]
                NC_ = 2306
                S.op("dve", lambda e, zi=zi, f=f: e.tensor_scalar(out=cvb[zi][:, 0:NC_], in0=zb[zi][:, 1:1 + NC_], scalar1=cwT[:, f, 1:2], scalar2=None, op0=ALU.mult),
                     reads=zres, writes=[f"cvb{zi}"])
                S.op("dve", lambda e, zi=zi, f=f: e.scalar_tensor_tensor(out=cvb[zi][:, 0:NC_], in0=zb[zi][:, 0:NC_], scalar=cwT[:, f, 0:1], in1=cvb[zi][:, 0:NC_], op0=ALU.mult, op1=ALU.add),
                     reads=zres + [f"cvb{zi}"], writes=[f"cvb{zi}"])
                S.op("dve", lambda e, zi=zi, f=f: e.scalar_tensor_tensor(out=cvb[zi][:, 0:NC_], in0=zb[zi][:, 2:2 + NC_], scalar=cwT[:, f, 2:3], in1=cvb[zi][:, 0:NC_], op0=ALU.mult, op1=ALU.add),
                     reads=zres + [f"cvb{zi}"], writes=[f"cvb{zi}"])
                for u_ in (16 + 2 * f, 17 + 2 * f):
                    mod_pe(0, u_)
                    if u_ + 1 < 48:
                        mod_dma(0, u_ + 1)
                if f >= 1:
                    b_part(f - 1)
                    if (f - 1) % 4 == 3:
                        while nload < 12 and nload < 3 * ((f - 1) // 4 + 1) + 4:
                            loaded[nload] = load_slab(win_d, slab_cols[nload], 512, KC, 4)
                            nload += 1
                if fl == 3 and nload < 12:
                    while nload < 12 and nload < 3 * g + 6:
                        loaded[nload] = load_slab(win_d, slab_cols[nload], 512, KC, 4)
                        nload += 1
            b_part(KC - 1)
            mod_finalize(0, 1)
            free_slabs()
            S.free("ctmp0", "ctmp1", "zb0", "zb1", "cvb0", "cvb1", "vst0", "vst1", "inT")
        elif active("l0_norm1"):
            S.free("inT")

        def load_inT_from(src, ntok, res_prefix):
            inT = S.carve("inT", IN_OFF, [KC, T], BF16)
            for ti, (t0, n) in enumerate(tblocks(ntok)):
                S.op("sp", lambda e, t0=t0, n=n: e.dma_start(out=inT[:, :, t0:t0 + n], in_=src[:, t0:t0 + n].rearrange("(kc p) t -> p kc t", p=128)),
                     reads=[(res_prefix, f) for f in range(KC)], writes=in_res(t0, n), dma_sem=f"inT{ti}")
            return inT

        if active("l0_wout"):
            inT = load_inT_from(vT, T, "vT")
            tbs3 = [(t0, t0, n) for (t0, n) in tblocks(T)]
            gemm_resid(0, wout_d, KC, 512, 4, inT, in_res, tbs3, 2, TMP2_OFF)
            S.free("inT")
        if active("l0_norm2"):
            mlp(0, T)

        if active("l1_norm1"):
            inT = std_norm(1, 0, T)
            if dbg_stop == "l1_norm1":
                dump_u(inT, T)
        if active("l1_qkv"):
            cosS = S.carve("cosS", TMP2_OFF, [SL], F32)
            sinS = S.carve("sinS", TMP2_OFF + 8192, [SL], F32)
            sp_load(cosS, cos_d, "cosS", "cosS")
            sp_load(sinS, sin_d, "sinS", "sinS")
            qsb = [S.carve(f"qsb{i}", TMP2_OFF + 16384 + i * 2048, [512], F32) for i in range(2)]
            t1 = [S.carve(f"t1{i}", TMP2_OFF + 20480 + i * 2048, [512], F32) for i in range(2)]
            t2 = [S.carve(f"t2{i}", TMP2_OFF + 24576 + i * 2048, [512], F32) for i in range(2)]
            qst = [S.carve(f"qst{i}", TMP2_OFF + 28672 + i * 4608, [T], BF16) for i in range(2)]
            vtst = [S.carve(f"vtst{i}", TMP2_OFF + 37888 + i * 1024, [512], BF16) for i in range(2)]
            loaded = {}
            for s in range(4):
                loaded[s] = load_slab(wqkv_d, s * 512, 512, KC, 4)
            tbs = tblocks(T)
            ri = 0
            rope_pend = {"f": None}

            def flush_rope():
                if rope_pend["f"] is not None:
                    f_ = rope_pend["f"]
                    rope_pend["f"] = None
                    f_()

            def store_head(hc, qs, is_k):
                if is_k:
                    kh = hc - 16
                    S.op("sp", lambda e: e.dma_start(out=kT[kh * 128:(kh + 1) * 128, :], in_=qst[qs]),
                         reads=[(f"qst{qs}", ti) for ti in range(5)], writes=[("kT", kh)], dma_sem=f"qst{qs}")
                else:
                    S.op("sp", lambda e: e.dma_start(out=qT[hc * 128:(hc + 1) * 128, :], in_=qst[qs][:, 0:SL]),
                         reads=[(f"qst{qs}", ti) for ti in range(4)], writes=[("qT", hc)], dma_sem=f"qst{qs}")

            for s in range(5):
                wname, wview = loaded[s]
                for c in range(4):
                    hc = s * 4 + c
                    is_k = hc >= 16
                    qs = hc % 2
                    ntb = 5 if is_k else 4
                    for ti in range(ntb):
                        t0, n = tbs[ti]
                        pb = next_bank(6)
                        mm_group(pb, wname, wview, c * 128, inT, KC, t0, n, in_res(t0, n))
                        if ti == 4:
                            S.op("act", lambda e, pb=pb, qs=qs, t0=t0, n=n: e.activation(out=qst[qs][:, t0:t0 + n], in_=bank(pb, n), func=AF.Copy),
                                 reads=[("ps", pb)], writes=[(f"qst{qs}", ti)])
                            continue
                        r = ri % 2
                        ri += 1
                        qscale = 1.0 if is_k else SCALE
                        S.op("act", lambda e, pb=pb, r=r, qscale=qscale: e.activation(out=qsb[r], in_=bank(pb), func=AF.Copy, scale=qscale),
                             reads=[("ps", pb)], writes=[f"qsb{r}"])
                        flush_rope()

                        def rope_tail(r=r, qs=qs, t0=t0, ti=ti, hc=hc, is_k=is_k):
                            pr = next_bank(6)
                            S.op("pe", lambda e: e.matmul(bank(pr), lhsT=rotR, rhs=qsb[r], start=True, stop=True),
                                 reads=["rotR", f"qsb{r}"], writes=[("ps", pr)])
                            S.op("dve", lambda e: e.tensor_tensor(out=t1[r], in0=qsb[r], in1=cosS[:, t0:t0 + 512], op=ALU.mult),
                                 reads=[f"qsb{r}", "cosS"], writes=[f"t1{r}"])
                            S.op("dve", lambda e: e.tensor_tensor(out=t2[r], in0=bank(pr), in1=sinS[:, t0:t0 + 512], op=ALU.mult),
                                 reads=[("ps", pr), "sinS"], writes=[f"t2{r}"])
                            S.op("dve", lambda e: e.tensor_tensor(out=qst[qs][:, t0:t0 + 512], in0=t1[r], in1=t2[r], op=ALU.add),
                                 reads=[f"t1{r}", f"t2{r}"], writes=[(f"qst{qs}", ti)])
                            if ti == 3:
                                store_head(hc, qs, is_k)
                        rope_pend["f"] = rope_tail
                if s + 4 < 6:
                    loaded[s + 4] = load_slab(wqkv_d, (s + 4) * 512, 512, KC, 4)
            flush_rope()
            wname, wview = loaded[5]
            for tt in range(18):
                pb = next_bank(6)

                def vmm(e, pb=pb, tt=tt, wview=wview):
                    for kc in range(KC):
                        inst = e.matmul(bank(pb), lhsT=inT[:, kc, tt * 128:(tt + 1) * 128], rhs=wview[:, kc, :], start=(kc == 0), stop=(kc == KC - 1))
                    return inst
                S.op("pe", vmm, reads=[wname] + in_res(tt * 128, 128), writes=[("ps", pb)])
                vs = tt % 2
                S.op("act", lambda e, pb=pb, vs=vs: e.activation(out=vtst[vs], in_=bank(pb), func=AF.Copy), reads=[("ps", pb)], writes=[f"vtst{vs}"])
                S.op("sp", lambda e, vs=vs, tt=tt: e.dma_start(out=vtok[tt * 128:(tt + 1) * 128, :], in_=vtst[vs]),
                     reads=[f"vtst{vs}"], writes=[("vtok", tt)], dma_sem=f"vtst{vs}")
            free_slabs()
            S.free("cosS", "sinS", "qsb0", "qsb1", "t10", "t11", "t20", "t21", "qst0", "qst1", "vtst0", "vtst1", "inT")
        elif active("l1_norm1"):
            S.free("inT")

        if active("l1_attn"):
            oT = S.carve("inT", IN_OFF, [KC, T], BF16)
            kS = S.carve("kS", W_OFF, [NKV, T], BF16)
            vS = S.carve("vS", W_OFF + 18432, [18, 512], BF16)
            S.op("sp", lambda e: e.dma_start(out=kS, in_=kT.rearrange("(g p) t -> p g t", p=128)),
                 reads=[("kT", g) for g in range(NKV)], writes=["kS"], dma_sem="kS")
            S.op("sp", lambda e: e.dma_start(out=vS, in_=vtok.rearrange("(tt p) c -> p tt c", p=128)),
                 reads=[("vtok", tt) for tt in range(18)], writes=["vS"], dma_sem="vS")
            qg = [S.carve(f"qg{i}", TMP2_OFF + i * 16384, [4, SL], BF16) for i in range(2)]
            pf = [S.carve(f"pf{i}", TMP2_OFF + 32768 + i * 2592, [648], F32) for i in range(2)]
            pbf = [S.carve(f"pbf{i}", TMP2_OFF + 37952 + i * 1280, [640], BF16) for i in range(2)]
            ptSb = [S.carve(f"ptS{i}", W_OFF + 36864 + i * 5120, [5, 512], BF16) for i in range(2)]
            stat = S.carve("stat", TMP2_OFF + 45632, [4, 8], F32)
            ones_f = S.carve("ones_f", TMP2_OFF + 45760, [128], F32)
            S.op("dve", lambda e: e.memset(ones_f, 1.0), writes=["ones_f"])
            PT_BANK = 4

            def ptH(b):
                return ps_t[:, b * 512:b * 512 + 320].bitcast(BF16).rearrange("p (kt q) -> p kt q", kt=5)
            OB = 7
            heads = [(g, n, hh) for g in range(NKV) for n in range(16) for hh in range(4)]
            NHD = len(heads)

            def load_q(g):
                gs = g % 2
                S.op("sp", lambda e: e.dma_start(out=qg[gs], in_=qT[g * 512:(g + 1) * 512, :].rearrange("(h p) t -> p h t", p=128)),
                     reads=[("qT", g * 4 + hh) for hh in range(4)], writes=[f"qg{gs}"], dma_sem=f"qg{gs}")

            def keyblocks(n):
                kb = [max(n - 1, 0), n, min(n + 1, 15)]
                mk = [2 if n == 0 else 0, None, 2 if n == 15 else 1]
                return kb, mk

            def st_A(i):
                g, n, hh = heads[i]
                if n == 0 and hh == 0 and g + 1 < NKV:
                    load_q(g + 1)
                gs = g % 2
                sb0 = 2 * (i % 2)
                kb, mk = keyblocks(n)

                def smm(e):
                    q_l = qg[gs][:, hh, n * 128:(n + 1) * 128]
                    A = ps_t[:, sb0 * 512:sb0 * 512 + 512]
                    Bk = ps_t[:, (sb0 + 1) * 512:(sb0 + 1) * 512 + 128]
                    if 0 < n < 15:
                        e.matmul(A[:, 0:384], lhsT=q_l, rhs=kS[:, g, (n - 1) * 128:(n + 2) * 128], start=True, stop=False, skip_group_check=True)
                    else:
                        for pos in range(3):
                            e.matmul(A[:, pos * 128:(pos + 1) * 128], lhsT=q_l, rhs=kS[:, g, kb[pos] * 128:(kb[pos] + 1) * 128],
                                     start=(pos == 0), stop=False, skip_group_check=True)
                    e.matmul(A[:, 384:512], lhsT=q_l, rhs=kS[:, g, SL:SL + 128], start=False, stop=False, skip_group_check=True)
                    for pos in (0, 2):
                        if mk[pos] is not None:
                            e.matmul(A[:, pos * 128:(pos + 1) * 128], lhsT=ident_b, rhs=masks[:, mk[pos], :], start=False, stop=(pos == 2), skip_group_check=True)
                    e.matmul(Bk, lhsT=q_l, rhs=kS[:, g, SL + 128:SL + 256], start=True, stop=False, skip_group_check=True)
                    hh_ = g * 4 + hh
                    return e.matmul(ps_t[:, (sb0 + 1) * 512 + 128:(sb0 + 1) * 512 + 129], lhsT=ones_f[0:1, :], rhs=sinkT[0:1, hh_:hh_ + 1],
                                    start=False, stop=True, skip_group_check=True)
                S.op("pe", smm, reads=[f"qg{gs}", "kS", "ident_b", "masks", "ones_f", "sinkT"], writes=[("ps", sb0), ("ps", sb0 + 1)])

            def st_B(i):
                sb0 = 2 * (i % 2)
                sps = ps_t[:, sb0 * 512:sb0 * 512 + 641]
                ss = i % 4
                st_ = stat[:, ss, :]
                S.op("dve", lambda e: e.reduce_max(out=st_[:, 0:1], in_=sps, axis=AX.X, negate=True),
                     reads=[("ps", sb0), ("ps", sb0 + 1)], writes=[("stat", ss, 0)])

            def st_C(i):
                sb0 = 2 * (i % 2)
                sl = i % 2
                sps = ps_t[:, sb0 * 512:sb0 * 512 + 641]
                ss = i % 4
                st_ = stat[:, ss, :]
                S.op("act", lambda e: e.activation(out=pf[sl][:, 0:641], in_=sps, func=AF.Exp, bias=st_[:, 0:1], scale=1.0, accum_out=st_[:, 1:2]),
                     reads=[("ps", sb0), ("ps", sb0 + 1), ("stat", ss, 0)], writes=[f"pf{sl}", ("stat", ss, 1)])

            def st_D(i):
                sl = i % 2
                ss = i % 4
                st_ = stat[:, ss, :]
                S.op("dve", lambda e: e.reciprocal(out=st_[:, 2:3], in_=st_[:, 1:2]),
                     reads=[("stat", ss, 1)], writes=[("stat", ss, 2)])
                S.op("pool", lambda e: e.tensor_scalar(out=pbf[sl], in0=pf[sl][:, 0:640], scalar1=st_[:, 2:3], scalar2=0.0, op0=ALU.mult, op1=ALU.add),
                     reads=[f"pf{sl}", ("stat", ss, 2)], writes=[f"pbf{sl}"])

            def st_E(i):
                g, n, hh = heads[i]
                sl = i % 2

                pbk = PT_BANK + i % 3
                ptv = ptH(pbk)

                def ptr(e):
                    for kt in range(5):
                        inst = e.transpose(ptv[:, kt, :], pbf[sl][:, kt * 128:(kt + 1) * 128], ident_b)
                    return inst
                S.op("pe", ptr, reads=[f"pbf{sl}", "ident_b"], writes=[("ps", pbk)])

            def st_V(i):
                g, n, hh = heads[i]
                pbk = PT_BANK + i % 3
                pk = (i // 4) % 2
                S.op("act", lambda e: e.activation(out=ptSb[pk][:, :, hh * 128:(hh + 1) * 128], in_=ptH(pbk), func=AF.Copy),
                     reads=[("ps", pbk)], writes=[(f"ptS{pk}", hh)])

            def st_F(i):
                g, n, hh = heads[i]
                kb, mk = keyblocks(n)
                pk = (i // 4) % 2

                def pv(e):
                    tiles = kb + [16, 17]
                    for kt in range(5):
                        inst = e.matmul(bank(OB), lhsT=vS[:, tiles[kt], g * 128:(g + 1) * 128], rhs=ptSb[pk][:, kt, :], start=(kt == 0), stop=(kt == 4))
                    return inst
                S.op("pe", pv, reads=["vS"] + [(f"ptS{pk}", h_) for h_ in range(4)], writes=[("ps", OB)])
                dsto = oT[:, g * 4:(g + 1) * 4, n * 128:(n + 1) * 128]
                pend["o"] = lambda: S.op("dve", lambda e: e.tensor_copy(out=dsto, in_=bank(OB).rearrange("p (h q) -> p h q", h=4)),
                                         reads=[("ps", OB)], writes=[("inT", g, n)])

            pend = {"o": None}
            load_q(0)
            st_A(0)
            st_A(1)
            st_B(0)
            for i in range(NHD):
                st_C(i)
                if pend["o"] is not None:
                    pend["o"]()
                    pend["o"] = None
                if i + 2 < NHD:
                    st_A(i + 2)
                if i + 1 < NHD:
                    st_B(i + 1)
                st_D(i)
                st_E(i)
                if i >= 3 and (i - 3) % 4 == 3:
                    st_F(i - 3)
                if i >= 2:
                    st_V(i - 2)
            st_V(NHD - 2)
            st_V(NHD - 1)
            st_F(NHD - 1)
            pend["o"]()
            if dbg_o is not None:
                S.op("sp", lambda e: e.dma_start(out=dbg_o.rearrange("(kc p) t -> p kc t", p=128), in_=oT[:, :, 0:SL]),
                     reads=[("inT", g, n) for g in range(4) for n in range(16)], writes=["dbg_o"], dma_sem="dbg")
            S.free("kS", "vS", "qg0", "qg1", "pf0", "pf1", "pbf0", "pbf1", "ptS0", "ptS1", "stat", "ones_f")
        if active("l1_wo"):
            tbs3 = [(t0, t0, n) for (t0, n) in tblocks(SL)]
            gemm_resid(1, wo_d, KC, 512, 4, oT, lambda r0, n: [("inT", g, nn) for g in range(4) for nn in range(r0 // 128, (r0 + n) // 128)],
                       tbs3, 2, TMP2_OFF)
        if active("l1_attn"):
            S.free("inT")
        if active("l1_norm2"):
            mlp(1, SL)

        if active("final"):
            fst = S.carve("fst", IN_OFF, [KC, 256], F32)
            osb = [S.carve(f"osb{i}", IN_OFF + 16384 + i * 8192, [D], F32) for i in range(2)]
            oi = {"i": 0}

            def final_tail(sb):
                for half in range(2):
                    tt = sb * 2 + half
                    o = oi["i"] % 2
                    oi["i"] += 1
                    for q in range(4):
                        pb = next_bank(6)

                        def tr(e, q=q, pb=pb, half=half):
                            for j in range(4):
                                inst = e.transpose(bank(pb)[:, j * 128:(j + 1) * 128], fst[:, 4 * q + j, half * 128:(half + 1) * 128], ident_f)
                            return inst
                        S.op("pe", tr, reads=[("fst", kc) for kc in range(4 * q, 4 * q + 4)] + ["ident_f"], writes=[("ps", pb)])
                        if q % 2 == 0:
                            S.op("dve", lambda e, o=o, q=q, pb=pb: e.tensor_copy(out=osb[o][:, q * 512:(q + 1) * 512], in_=bank(pb)),
                                 reads=[("ps", pb)], writes=[(f"osb{o}", q)])
                        else:
                            S.op("act", lambda e, o=o, q=q, pb=pb: e.activation(out=osb[o][:, q * 512:(q + 1) * 512], in_=bank(pb), func=AF.Copy),
                                 reads=[("ps", pb)], writes=[(f"osb{o}", q)])
                    S.op("sp", lambda e, o=o, tt=tt: e.dma_start(out=y_d[tt * 128:(tt + 1) * 128, :], in_=osb[o]),
                         reads=[(f"osb{o}", q) for q in range(4)], writes=[("y", tt)], dma_sem=f"osb{o}")

            sqf = S.carve("sqf", IN_OFF + 32768, [KC, 256], BF16)
            norm_phase(1, 0, SL, lambda kc, t0: fst[:, kc, :], lambda kc, t0: ("fst", kc), final=True,
                       sq_fn=lambda sb: (sqf, ["sqf"]))
            S.wait_all("sp", ["osb0", "osb1"])
        else:
            S.wait_all("sp", [s for s in S.cnt if not s.startswith("e_")])
        S.emit()
    return nc


def _fm(v):
    v = np.asarray(v, np.float32)
    return np.ascontiguousarray(v.reshape(-1, 128).T)


def _rope_tables():
    rows_n = SL // 64
    row = np.repeat(np.arange(rows_n), 64).astype(np.float32)
    col = np.tile(np.arange(64), rows_n).astype(np.float32)
    nf = 32
    inv_freq = (np.float32(10000.0) ** (-np.arange(nf, dtype=np.float32) / np.float32(nf))).astype(np.float32)
    ang_r = row[:, None] * inv_freq[None, :]
    ang_c = col[:, None] * inv_freq[None, :]
    ang = np.concatenate([ang_r, ang_r, ang_c, ang_c], axis=-1).astype(np.float32)
    return np.ascontiguousarray(np.cos(ang).T.astype(np.float32)), np.ascontiguousarray(np.sin(ang).T.astype(np.float32))


def _consts():
    cosT, sinT = _rope_tables()
    R = np.zeros((128, 128), np.float32)
    for d_out in range(128):
        half = (d_out // 32) % 2
        if half == 0:
            R[d_out + 32, d_out] = -1.0
        else:
            R[d_out - 32, d_out] = 1.0
    ii = np.arange(128)[:, None]
    jj = np.arange(128)[None, :]
    m_prev = np.where(jj >= ii, 0.0, MASKV)
    m_next = np.where(jj <= ii, 0.0, MASKV)
    m_full = np.full((128, 128), MASKV)
    masks = np.stack([m_prev, m_next, m_full], axis=1).astype(np.float32).reshape(128, 384)
    return cosT, sinT, R, masks, np.eye(128, dtype=np.float32)


def make_in_maps(inputs, cores):
    f = {k: np.asarray(v) for k, v in inputs.items()}
    cosT, sinT, R, masks, ident = _consts()
    shared = {
        "mod_w": np.ascontiguousarray(f["mod_w"], np.float32),
        "mod_bT": np.ascontiguousarray(np.concatenate([_fm(f["mod_b"][0]), _fm(f["mod_b"][1])], axis=1)),
        "n1T": np.ascontiguousarray(np.concatenate([_fm(f["norm1_g"][0]), _fm(f["norm1_g"][1])], axis=1)),
        "n2T": np.ascontiguousarray(np.concatenate([_fm(f["norm2_g"][0]), _fm(f["norm2_g"][1])], axis=1)),
        "fgT": _fm(f["final_g"]),
        "conv_w_in": np.ascontiguousarray(f["conv_w_in"][0], np.float32),
        "conv_wT": np.ascontiguousarray(np.stack([_fm(f["conv_w"][0, t]) for t in range(3)], axis=2).reshape(128, KC * 3)),
        "conv_w_out": np.ascontiguousarray(f["conv_w_out"][0], np.float32),
        "w_qkv": np.ascontiguousarray(f["attn_w_qkv"][0], np.float32),
        "sink": np.ascontiguousarray(np.broadcast_to(f["attn_sink"][0].astype(np.float32)[None, :], (128, NH))),
        "w_o": np.ascontiguousarray(f["attn_w_o"][0], np.float32),
        "w1": np.ascontiguousarray(f["mlp_w1"], np.float32),
        "w2": np.ascontiguousarray(f["mlp_w2"], np.float32),
        "cosT": cosT, "sinT": sinT, "rotR": R, "masks": masks, "ident": ident,
    }
    maps = []
    cc = _fm(f["c_ctx"])
    for b in cores:
        m = dict(shared)
        m["x"] = np.ascontiguousarray(f["x"][b], np.float32)
        m["ctx"] = np.ascontiguousarray(f["ctx"][b], np.float32)
        m["cT"] = np.ascontiguousarray(np.stack([_fm(f["c"][b]), cc], axis=2).reshape(128, KC * 2))
        maps.append(m)
    return maps


_NC_CACHE = {}


def kernel(**inputs):
    if "nc" not in _NC_CACHE:
        _NC_CACHE["nc"] = build()
    nc = _NC_CACHE["nc"]
    in_maps = make_in_maps(inputs, list(range(8)))
    res = run_bass_kernel_spmd(nc, in_maps, core_ids=list(range(8)))
    return np.stack([np.asarray(r["y"], np.float32) for r in res.results], axis=0)
```
